# Optimizing a Trainium2 kernel written in Bass

```python
import math
import jax, jax.numpy as jnp
from jax import lax
import numpy as np

D_MODEL = 1024
BATCH = 4
SEQ = 8192
DEPTH = 4

N_META = 16
POOL_WIDTH = D_MODEL // 2
POOL_WINDOWS = (2, 4, 8, 16)
POOL_GROUPS = len(POOL_WINDOWS)
POOL_GROUP_DIM = POOL_WIDTH // POOL_GROUPS
N_HEADS = 8
QK_NOPE_DIM = 64
QK_ROPE_DIM = 32
QK_HEAD_DIM = QK_NOPE_DIM + QK_ROPE_DIM
V_HEAD_DIM = 64
MLA_WIDTH = N_HEADS * V_HEAD_DIM
KV_LORA_RANK = 256
Q_LORA_RANK = 768
ROPE_THETA = 10000.0
NORM_EPS = 1e-6
Q_BLOCK = 128
MASK_VALUE = -1e30

IN_SPLITS = (POOL_WIDTH, POOL_WIDTH, Q_LORA_RANK, KV_LORA_RANK, QK_ROPE_DIM, MLA_WIDTH, D_MODEL, D_MODEL)
D_IN = sum(IN_SPLITS)
IN_SPLIT_POINTS = tuple(int(v) for v in np.cumsum(IN_SPLITS)[:-1])

kernel_name = "hybrid_pool_mla_gated_trunk"


def rmsnorm(x, gain):
    xf = x.astype(jnp.float32)
    inv = lax.rsqrt(jnp.mean(xf * xf, axis=-1, keepdims=True) + NORM_EPS)
    return (xf * inv * gain.astype(jnp.float32)).astype(x.dtype)


def apply_rope(x, pos):
    half = x.shape[-1] // 2
    inv_freq = ROPE_THETA ** (-jnp.arange(half, dtype=jnp.float32) / half)
    ang = pos.astype(jnp.float32)[..., None] * inv_freq
    cos = jnp.cos(ang)[:, :, None, :]
    sin = jnp.sin(ang)[:, :, None, :]
    xf = x.astype(jnp.float32)
    x1, x2 = xf[..., :half], xf[..., half:]
    return jnp.concatenate([x1 * cos - x2 * sin, x2 * cos + x1 * sin], axis=-1).astype(x.dtype)


def pool_mix(u, w_group, scale):
    B, L, _ = u.shape
    ug = u.reshape(B, L, POOL_GROUPS, POOL_GROUP_DIM).astype(jnp.float32)
    csum = jnp.cumsum(ug, axis=1)
    t1 = jnp.arange(1, L + 1, dtype=jnp.float32)
    means = []
    for g, w in enumerate(POOL_WINDOWS):
        s = csum[:, :, g]
        lag = jnp.pad(s, ((0, 0), (w, 0), (0, 0)))[:, :L]
        cnt = jnp.minimum(t1, float(w))[None, :, None]
        means.append((s - lag) / cnt)
    mixed = (jnp.stack(means, axis=2) - ug).astype(u.dtype)
    y = jnp.einsum('blgc,gcd->blgd', mixed, w_group)
    return y.reshape(B, L, POOL_WIDTH) * scale


def causal_block_attention(q, k, v):
    B, L = q.shape[0], q.shape[1]
    pad_front = (-N_META) % Q_BLOCK
    pad_back = (-(L + pad_front)) % Q_BLOCK
    padw = ((0, 0), (pad_front, pad_back), (0, 0), (0, 0))
    q, k, v = jnp.pad(q, padw), jnp.pad(k, padw), jnp.pad(v, padw)
    n_blocks = q.shape[1] // Q_BLOCK
    scale = 1.0 / math.sqrt(QK_HEAD_DIM)
    outs = []
    for i in range(n_blocks):
        q0 = i * Q_BLOCK
        kend = q0 + Q_BLOCK
        s = jnp.einsum('bqhd,bkhd->bhqk', q[:, q0:kend], k[:, :kend]).astype(jnp.float32) * scale
        qi = jnp.arange(q0, kend)[:, None]
        ki = jnp.arange(kend)[None, :]
        valid = (ki <= qi) & (ki >= pad_front)
        s = jnp.where(valid, s, MASK_VALUE)
        p = jax.nn.softmax(s, axis=-1).astype(v.dtype)
        outs.append(jnp.einsum('bhqk,bkhd->bqhd', p, v[:, :kend]))
    o = jnp.concatenate(outs, axis=1)
    return o[:, pad_front:pad_front + L]


def mla(c_q_raw, c_kv_raw, k_rope_raw, pos, g_qa, g_kva, w_q_b, w_kv_b, g_qn, g_kn):
    B, L, _ = c_q_raw.shape
    c_q = rmsnorm(c_q_raw, g_qa)
    c_kv = rmsnorm(c_kv_raw, g_kva)
    q = (c_q @ w_q_b).reshape(B, L, N_HEADS, QK_HEAD_DIM)
    kv = (c_kv @ w_kv_b).reshape(B, L, N_HEADS, QK_NOPE_DIM + V_HEAD_DIM)
    k_nope, v = kv[..., :QK_NOPE_DIM], kv[..., QK_NOPE_DIM:]
    k_pe = jnp.broadcast_to(k_rope_raw[:, :, None, :], (B, L, N_HEADS, QK_ROPE_DIM))
    k = jnp.concatenate([k_nope, k_pe], axis=-1)
    q = rmsnorm(q, g_qn)
    k = rmsnorm(k, g_kn)
    q = jnp.concatenate([q[..., :QK_NOPE_DIM], apply_rope(q[..., QK_NOPE_DIM:], pos)], axis=-1)
    k = jnp.concatenate([k[..., :QK_NOPE_DIM], apply_rope(k[..., QK_NOPE_DIM:], pos)], axis=-1)
    o = causal_block_attention(q, k, v)
    return o.reshape(B, L, MLA_WIDTH)


def setup_inputs(seed: int = 0) -> dict:
    key = jax.random.key(seed)
    ks = jax.random.split(key, 18)
    f32 = jnp.float32

    def nrm(k, shape, scale):
        return jax.random.normal(k, shape, f32) * scale

    def gain(k, shape):
        return 1.0 + 0.02 * jax.random.normal(k, shape, f32)

    x = jax.random.normal(ks[0], (BATCH, SEQ, D_MODEL), f32)
    offset = jax.random.randint(ks[1], (BATCH, 1), 0, 4096, dtype=jnp.int32)
    positions = offset + jnp.arange(SEQ, dtype=jnp.int32)[None, :]
    return {
        "x": x,
        "positions": positions,
        "meta_tokens": nrm(ks[2], (N_META, D_MODEL), 1.0),
        "norm_gain": gain(ks[3], (DEPTH, D_MODEL)),
        "w_in": nrm(ks[4], (DEPTH, D_MODEL, D_IN), D_MODEL ** -0.5),
        "pool_w_group": nrm(ks[5], (DEPTH, POOL_GROUPS, POOL_GROUP_DIM, POOL_GROUP_DIM), POOL_GROUP_DIM ** -0.5),
        "pool_scale": gain(ks[6], (DEPTH, POOL_WIDTH)),
        "pool_w_up": nrm(ks[7], (DEPTH, POOL_WIDTH, D_MODEL), POOL_WIDTH ** -0.5),
        "q_a_norm_gain": gain(ks[8], (DEPTH, Q_LORA_RANK)),
        "kv_a_norm_gain": gain(ks[9], (DEPTH, KV_LORA_RANK)),
        "w_q_b": nrm(ks[10], (DEPTH, Q_LORA_RANK, N_HEADS * QK_HEAD_DIM), Q_LORA_RANK ** -0.5),
        "w_kv_b": nrm(ks[11], (DEPTH, KV_LORA_RANK, N_HEADS * (QK_NOPE_DIM + V_HEAD_DIM)), KV_LORA_RANK ** -0.5),
        "q_norm_gain": gain(ks[12], (DEPTH, QK_HEAD_DIM)),
        "k_norm_gain": gain(ks[13], (DEPTH, QK_HEAD_DIM)),
        "mla_w_up": nrm(ks[14], (DEPTH, MLA_WIDTH, D_MODEL), MLA_WIDTH ** -0.5),
        "w_out": nrm(ks[15], (DEPTH, D_MODEL, D_MODEL), (D_MODEL * 2 * DEPTH) ** -0.5),
    }


def reference(x, positions, meta_tokens, norm_gain, w_in, pool_w_group, pool_scale, pool_w_up,
              q_a_norm_gain, kv_a_norm_gain, w_q_b, w_kv_b, q_norm_gain, k_norm_gain, mla_w_up, w_out):
    B = x.shape[0]
    meta = jnp.broadcast_to(meta_tokens[None].astype(x.dtype), (B, N_META, D_MODEL))
    h_res = jnp.concatenate([meta, x], axis=1)
    meta_pos = jnp.broadcast_to(jnp.arange(N_META, dtype=jnp.int32)[None], (B, N_META))
    pos = jnp.concatenate([meta_pos, positions + N_META], axis=1)

    for l in range(DEPTH):
        h = rmsnorm(h_res, norm_gain[l])
        proj = h @ w_in[l]
        u_pool, z_pool, c_q, c_kv, k_rope, z_mla, g_pool, g_mla = jnp.split(proj, IN_SPLIT_POINTS, axis=-1)
        y_pool = (pool_mix(u_pool, pool_w_group[l], pool_scale[l]) * jax.nn.silu(z_pool)) @ pool_w_up[l]
        o_mla = mla(c_q, c_kv, k_rope, pos, q_a_norm_gain[l], kv_a_norm_gain[l], w_q_b[l], w_kv_b[l],
                    q_norm_gain[l], k_norm_gain[l])
        y_mla = (o_mla * jax.nn.silu(z_mla)) @ mla_w_up[l]
        merged = jax.nn.sigmoid(g_pool) * y_pool + jax.nn.sigmoid(g_mla) * y_mla
        h_res = h_res + merged @ w_out[l]

    return h_res[:, N_META:]
```

```python
import math
from contextlib import ExitStack

import numpy as np
import concourse.bass as bass
import concourse.mybir as mybir
from concourse.bass_utils import run_bass_kernel_spmd

F32 = mybir.dt.float32
BF16 = mybir.dt.bfloat16
I32 = mybir.dt.int32
AF = mybir.ActivationFunctionType
ALU = mybir.AluOpType

D = 1024
DEPTH = 4
BATCH = 4
SEQ = 8192
NMETA = 16
NH = 8
D_IN = 4640
EPS = 1e-6
TW = 512
POOL_W = (2, 4, 8, 16)
WSLOT = 1536
C_U, C_ZP, C_CQ, C_CKV, C_KR, C_ZM, C_GP, C_GM = 0, 512, 1024, 1792, 2048, 2080, 2592, 3616


class Sem:
    def __init__(self, h, owner=None):
        self.h = h
        self.cnt = 0
        self.owner = owner


class Buf:
    __slots__ = ("w", "r", "name")

    def __init__(self, name=""):
        self.w = None
        self.r = {}
        self.name = name


class Eng:
    def __init__(self, k, h, name):
        self.k = k
        self.h = h
        self.name = name
        self.sem = None
        self.known = {}
        self.new_epoch()

    def new_epoch(self):
        self.sem = self.k.new_sem(self.name, owner=self.name)

    def needs(self, ev):
        if ev is None:
            return False
        s, v = ev
        if self.known.get(s, 0) >= v:
            return False
        if self.name == "pe" and s.owner == "pe":
            return False
        return True

    def wait(self, ev):
        if self.needs(ev):
            s, v = ev
            self.h.wait_ge(s.h, v)
            self.known[s] = v

    def pending(self, evs):
        need = {}
        for ev in evs:
            if self.needs(ev):
                s, v = ev
                if need.get(s, 0) < v:
                    need[s] = v
        for s, v in need.items():
            self.known[s] = v
        return list(need.items())

    def issue(self, fn, evs):
        pend = self.pending(evs)
        for s, v in pend[:-1]:
            self.h.wait_ge(s.h, v)
        ins = fn()
        if pend:
            s, v = pend[-1]
            ins._wait_ge(s.h, v)
        return ins


class K:
    def __init__(self, nc, stack):
        self.nc = nc
        self.stack = stack
        self.nsem = 0
        self.all_sems = []
        self.pe = Eng(self, nc.tensor, "pe")
        self.act = Eng(self, nc.scalar, "act")
        self.dve = Eng(self, nc.vector, "dve")
        self.pool = Eng(self, nc.gpsimd, "pool")
        self.sp = Eng(self, nc.sync, "sp")
        self.store_sems = [self.new_sem("st%d" % i) for i in range(8)]
        self.store_i = 0

    def new_sem(self, name, owner=None):
        self.nsem += 1
        sm = Sem(self.stack.enter_context(self.nc.semaphore("%s_%d" % (name, self.nsem))), owner)
        self.all_sems.append(sm)
        return sm

    def epoch(self):
        for e in (self.pe, self.act, self.dve, self.pool, self.sp):
            e.new_epoch()

    def sb(self, name, shape, dt):
        return self.stack.enter_context(self.nc.sbuf_tensor(name, list(shape), dt))

    def ps(self, name):
        return self.stack.enter_context(self.nc.psum_tensor(name, [128, 512], F32))

    @staticmethod
    def _deps(reads, writes):
        evs = []
        for b in reads:
            evs.append(b.w)
        for b in writes:
            evs.append(b.w)
            evs.extend(b.r.items())
        return evs

    @staticmethod
    def _update(ev, reads, writes):
        s, v = ev
        for b in reads:
            if b.r.get(s, 0) < v:
                b.r[s] = v
        for b in writes:
            b.w = ev
            b.r = {}

    def op(self, eng, fn, reads=(), writes=()):
        ins = eng.issue(fn, self._deps(reads, writes))
        eng.sem.cnt += 1
        ins.then_inc(eng.sem.h, 1)
        ev = (eng.sem, eng.sem.cnt)
        self._update(ev, reads, writes)
        return ev

    def dma(self, q, out, in_, reads=(), writes=(), sem=None, extra=(), **kw):
        evs = self._deps(reads, writes) + list(extra)
        if sem is None:
            sem = self.store_sems[self.store_i % len(self.store_sems)]
            self.store_i += 1
            if sem.cnt:
                evs.append((sem, sem.cnt))
        ins = q.issue(lambda: q.h.dma_start(out=out, in_=in_, **kw), evs)
        sem.cnt += 16
        ins.then_inc(sem.h, 16)
        ev = (sem, sem.cnt)
        self._update(ev, reads, writes)
        return ev


class Ring:
    def __init__(self, items):
        self.items = items
        self.i = 0

    def next(self):
        it = self.items[self.i % len(self.items)]
        self.i += 1
        return it


def chunk_plan():
    p = []
    p += [("ckv0", 1024), ("ckv1", 1024), ("kr", 1536), ("kvbk", 1024), ("kvbv", 1024)]
    p += [("cq%d" % j, 1024) for j in range(6)]
    p += [("qb%d" % h, 1152) for h in range(NH)]
    p += [("zm%d" % c, 1024) for c in range(4)]
    p += [("u%d" % g, 1024) for g in range(4)]
    p += [("zp%d" % g, 1024) for g in range(4)]
    p += [("pg", 512)]
    for m in range(8):
        p += [("gp%d" % m, 1024), ("gm%d" % m, 1024), ("up%d" % m, 1024)]
    p += [("wo%d" % m, 1024) for m in range(8)]
    return p


def build(n_layers=DEPTH, n_tiles=SEQ // TW, stage=3.7, meta_out=False):
    nc = bass.Bass("TRN2", target_bir_lowering=False)
    S = n_tiles * TW
    L = NMETA + S
    NKB = 1 + 4 * n_tiles
    plan = chunk_plan()
    cidx = {n: i for i, (n, _) in enumerate(plan)}
    NCH = len(plan)

    def din(name, shape, dt=F32):
        return nc.dram_tensor(name, list(shape), dt, kind="ExternalInput").ap()

    xT = din("xT", [D, S])
    pos_in = din("pos", [1, S], I32)
    metaT = din("metaT", [D, NMETA])
    norm_gain = din("norm_gain", [n_layers, D])
    w_in = din("w_in", [n_layers, D, D_IN])
    pool_w_group = din("pool_w_group", [n_layers, 4, 128, 128])
    pool_scale = din("pool_scale", [n_layers, 512])
    pool_w_up = din("pool_w_up", [n_layers, 512, D])
    q_a_norm_gain = din("q_a_norm_gain", [n_layers, 768])
    kv_a_norm_gain = din("kv_a_norm_gain", [n_layers, 256])
    w_q_b = din("w_q_b", [n_layers, 768, 768])
    w_kv_b = din("w_kv_b", [n_layers, 256, 1024])
    q_norm_gain = din("q_norm_gain", [n_layers, 96])
    k_norm_gain = din("k_norm_gain", [n_layers, 96])
    mla_w_up = din("mla_w_up", [n_layers, 512, D])
    w_out = din("w_out", [n_layers, D, D])
    cst = din("cst", [128, 160])
    yT = nc.dram_tensor("yT", [D, S], F32, kind="ExternalOutput").ap()
    metaT_out = nc.dram_tensor("metaT_out", [D, NMETA], F32, kind="ExternalOutput").ap() if meta_out else None

    def dscr(name, shape, dt):
        return nc.dram_tensor(name, list(shape), dt, kind="Internal").ap()

    wsc = dscr("wsc", [n_layers, NCH, 128, WSLOT], BF16)
    xs_meta = dscr("xs_meta", [128, 8, NMETA], F32)
    xs = dscr("xs", [n_tiles, 128, 8, TW], F32)
    kT_d = dscr("kT_d", [NKB, 96, NH, 128], BF16)
    v_d = dscr("v_d", [NKB, 128, NH, 128], BF16)
    cos_d = dscr("cos_d", [32, L], F32)
    sin_d = dscr("sin_d", [32, L], F32)

    with ExitStack() as stack:
        k = K(nc, stack)
        pe, act, dve, pool, sp = k.pe, k.act, k.dve, k.pool, k.sp
        sb = k.sb

        cst_t = sb("cst_t", [128, 160], F32)
        ones_bf = sb("ones_bf", [128, 128], BF16)
        tri_bf = sb("tri_bf", [128, 128], BF16)
        gn_t = sb("gn_t", [128, n_layers, 8], F32)
        gqa_t = sb("gqa_t", [128, n_layers, 6], F32)
        gkva_t = sb("gkva_t", [128, n_layers, 2], F32)
        psc_t = sb("psc_t", [128, n_layers, 4], F32)
        gq_t = sb("gq_t", [96, n_layers, 2], F32)
        gk_t = sb("gk_t", [96, n_layers, 2], F32)
        xt = [sb("xt%d" % i, [128, 8, TW], F32) for i in range(2)]
        hT = sb("hT", [128, 8, TW], BF16)
        sq = [sb("sq%d" % i, [128, TW], BF16) for i in range(2)]
        sqk = [sb("sqk%d" % i, [96, TW], BF16) for i in range(2)]
        rs = [sb("rs%d" % i, [128, TW], F32) for i in range(3)]
        ckv_raw = sb("ckv_raw", [128, 2, TW], F32)
        ckvn = sb("ckvn", [128, 2, TW], BF16)
        kr_raw = sb("kr_raw", [96, 2, TW], F32)
        kR = sb("kR", [96, TW], F32)
        tmp32 = [sb("tmp32_%d" % i, [96, TW], F32) for i in range(2)]
        cs_t2 = [sb("cs_t%d" % i, [96, 2, TW], F32) for i in range(2)]
        cg_t = sb("cg_t", [96, 4, TW], F32)
        kT_t = sb("kT_t", [96, 4, NH, 128], BF16)
        va_t = sb("va_t", [128, 4, NH, 128], BF16)
        cqn = sb("cqn", [128, 6, TW], BF16)
        qT = sb("qT", [96, NH, TW], BF16)
        cq_raw = sb("cq_raw", [128, 6, TW], F32)
        zm = sb("zm", [128, 4, TW], F32)
        zp = cq_raw[:, 0:4, :]
        u_t = sb("u_t", [128, 4, 16 + TW], F32)
        pt = [sb("pt%d" % i, [128, 16 + TW], F32) for i in range(2)]
        pg = sb("pg", [128, 4, TW], BF16)
        om = sb("om", [128, 4, TW], BF16)
        rec = [sb("rec%d" % i, [128, TW], F32) for i in range(2)]
        otmp = [sb("otmp%d" % i, [128, TW], F32) for i in range(1)] * 2
        sg = [sb("sg%d" % i, [128, TW], F32) for i in range(4)]
        mt = [sb("mt%d" % i, [128, TW], F32) for i in range(2)]
        merged = sb("merged", [128, 8, TW], BF16)
        pbuf = [sb("pbuf%d" % i, [128, TW], BF16) for i in range(4)]
        mixed = pbuf
        posf = rs[0][0:32]
        ang = [rs[1][0:32], rs[2][0:32]]
        nq = tmp32[0][0:32]
        posi = tmp32[1][0:32].bitcast(I32)
        NW = 5
        wring = [sb("wr%d" % i, [128, WSLOT], BF16) for i in range(NW)]
        NKV = 4
        kslot = [sb("ks%d" % i, [96, 4, 128], BF16) for i in range(NKV)]
        vslot = [sb("vs%d" % i, [128, 4, 128], BF16) for i in range(NKV)]
        banks = [k.ps("bank%d" % i) for i in range(8)]

        def bl(name, n):
            return [Buf("%s%d" % (name, i)) for i in range(n)]

        b_cst, b_ones, b_tri, b_gains = Buf(), Buf(), Buf(), Buf()
        b_xt = [bl("xt", 8), bl("xt", 8)]
        b_hT = bl("hT", 8)
        b_sq, b_sqk = bl("sq", 2), bl("sqk", 2)
        b_sqk_rope = bl("sqkr", 2)
        b_rs = bl("rs", 3)
        b_ckv_raw, b_ckvn = bl("ckvr", 2), bl("ckvn", 2)
        b_kr_raw, b_kR = Buf(), Buf()
        b_tmp32 = bl("tmp32", 2)
        b_cs2, b_cg = [Buf(), Buf()], Buf()
        b_kT, b_va = Buf(), Buf()
        b_va_ones = Buf()
        b_cq_raw, b_cqn = bl("cqr", 6), bl("cqn", 6)
        b_qT = bl("qT", NH)
        b_zm = bl("zm", 4)
        b_zp = b_cq_raw[0:4]
        b_u, b_uh = bl("u", 4), bl("uh", 4)
        b_pt = bl("pt", 2)
        b_pg, b_om = bl("pg", 4), bl("om", 4)
        b_rec, b_otmp = bl("rec", 2), bl("otmp", 1) * 2
        b_sg, b_mt = bl("sg", 4), bl("mt", 2)
        b_merged = bl("merged", 8)
        b_pbuf = bl("pbuf", 4)
        b_mixed = b_pbuf
        b_wring, b_kslot, b_vslot = bl("wr", NW), bl("ks", NKV), bl("vs", NKV)
        b_banks = bl("bank", 8)
        b_posi, b_posf, b_ang = b_tmp32[1], b_rs[0], [b_rs[1], b_rs[2]]
        b_nq = b_tmp32[0]
        b_wsc = [[Buf() for _ in range(NCH)] for _ in range(n_layers)]
        b_xs_meta = Buf()
        b_xs = bl("xs", n_tiles)
        b_kvd = bl("kvd", n_tiles + 1)
        b_csd = bl("csd", n_tiles + 1)
        wsem = [k.new_sem("w") for _ in range(NW)]
        ksem = [k.new_sem("ks") for _ in range(NKV)]
        vsem = [k.new_sem("vs") for _ in range(NKV)]
        ld_rings = {"sp": Ring([k.new_sem("ld") for _ in range(4)]), "pool": Ring([k.new_sem("lq") for _ in range(2)])}

        def ld(q, out, in_, reads=(), writes=(), **kw):
            s = ld_rings[q.name].next()
            return k.dma(q, out, in_, reads=reads, writes=writes, sem=s, extra=[(s, s.cnt)] if s.cnt else [], **kw)

        ld(sp, cst_t[:], cst[:, :], writes=[b_cst])
        k.op(dve, lambda: nc.vector.memset(ones_bf[:], 1.0), writes=[b_ones])
        k.op(dve, lambda: nc.vector.tensor_copy(out=tri_bf[:], in_=cst_t[:, 0:128]), reads=[b_cst], writes=[b_tri])
        k.op(dve, lambda: nc.vector.memset(va_t[:], 1.0), writes=[b_va, b_va_ones])
        k.op(dve, lambda: nc.vector.memset(kT_t[:], 0.0), writes=[b_kT])
        for i_ in range(2):
            k.op(dve, lambda i_=i_: nc.vector.memset(pt[i_][:], 0.0), writes=[b_pt[i_]])
        for g in range(4):
            k.op(dve, lambda g=g: nc.vector.memset(u_t[:, g, 0:16], 0.0), writes=[b_uh[g]])

        def colload(dst, src2d, nchunk):
            ld(pool, dst, src2d.rearrange("l (c p o) -> p l c o", p=128, o=1)[:, :, :, 0], writes=[b_gains],
               allow_slow_non_contiguous=True)

        colload(gn_t[:], norm_gain, 8)
        colload(gqa_t[:], q_a_norm_gain, 6)
        colload(gkva_t[:], kv_a_norm_gain, 2)
        colload(psc_t[:], pool_scale, 4)
        for (gt, src) in ((gq_t, q_norm_gain), (gk_t, k_norm_gain)):
            srcT = src.rearrange("l (d o) -> d l o", o=1)[:, :, 0]
            ld(pool, gt[0:96, :, 0], srcT[0:96, :], writes=[b_gains], allow_slow_non_contiguous=True)
            ld(pool, gt[64:80, :, 1], srcT[80:96, :], writes=[b_gains], allow_slow_non_contiguous=True)
            ld(pool, gt[80:96, :, 1], srcT[64:80, :], writes=[b_gains], allow_slow_non_contiguous=True)

        def cast(l, name, col0, out_view, in_view):
            ci = cidx[name]
            k.dma(pool, out_view(wsc[l, ci]), in_view, writes=[b_wsc[l][ci]])

        def emit_casts(l):
            def std(name, w2d, c0, kc, m, off=0, mtot=None):
                mtot_ = mtot or m
                ci = cidx[name]
                o = wsc[l, ci][:, 0:kc * mtot_].rearrange("p (c m) -> p c m", m=mtot_)[:, :, off:off + m]
                i = w2d.rearrange("(c p) n -> p c n", p=128)[:, :, c0:c0 + m]
                k.dma(pool, o, i, writes=[b_wsc[l][ci]])

            wi = w_in[l]
            std("ckv0", wi, C_CKV, 8, 128)
            std("ckv1", wi, C_CKV + 128, 8, 128)
            std("kr", wi, C_CKV, 8, 64, off=0, mtot=192)
            std("kr", wi, C_KR, 8, 32, off=64, mtot=192)
            std("kr", wi, C_CKV, 8, 64, off=96, mtot=192)
            std("kr", wi, C_KR + 16, 8, 16, off=160, mtot=192)
            std("kr", wi, C_KR, 8, 16, off=176, mtot=192)
            wkv = w_kv_b[l]
            for h in range(NH):
                ci = cidx["kvbk"]
                ov = wsc[l, ci][:, 0:2 * NH * 64].rearrange("p (c h m) -> p c h m", h=NH, m=64)
                iv = wkv.rearrange("(c p) n -> p c n", p=128)
                k.dma(pool, ov[:, :, h, :], iv[:, :, h * 128:h * 128 + 64], writes=[b_wsc[l][ci]])
                ci = cidx["kvbv"]
                ov = wsc[l, ci][:, 0:2 * 512].rearrange("p (c h m) -> p c h m", h=NH, m=64)
                k.dma(pool, ov[:, :, h, :], iv[:, :, h * 128 + 64:h * 128 + 128], writes=[b_wsc[l][ci]])
            for j in range(6):
                std("cq%d" % j, wi, C_CQ + 128 * j, 8, 128)
            wq = w_q_b[l]
            for h in range(NH):
                n = "qb%d" % h
                std(n, wq, h * 96, 6, 96, off=0, mtot=192)
                std(n, wq, h * 96, 6, 64, off=96, mtot=192)
                std(n, wq, h * 96 + 80, 6, 16, off=160, mtot=192)
                std(n, wq, h * 96 + 64, 6, 16, off=176, mtot=192)
            for c in range(4):
                std("zm%d" % c, wi, C_ZM + 128 * c, 8, 128)
            for g in range(4):
                std("u%d" % g, wi, C_U + 128 * g, 8, 128)
                std("zp%d" % g, wi, C_ZP + 128 * g, 8, 128)
            ci = cidx["pg"]
            k.dma(pool, wsc[l, ci][:, 0:512].rearrange("p (g m) -> p g m", m=128),
                  pool_w_group[l].rearrange("g p m -> p g m"), writes=[b_wsc[l][ci]])
            for m in range(8):
                std("gp%d" % m, wi, C_GP + 128 * m, 8, 128)
                std("gm%d" % m, wi, C_GM + 128 * m, 8, 128)
                ci = cidx["up%d" % m]
                ov = wsc[l, ci][:, 0:1024].rearrange("p (t c m) -> p t c m", t=2, m=128)
                k.dma(pool, ov[:, 0], pool_w_up[l].rearrange("(c p) n -> p c n", p=128)[:, :, 128 * m:128 * m + 128],
                      writes=[b_wsc[l][ci]])
                k.dma(pool, ov[:, 1], mla_w_up[l].rearrange("(c p) n -> p c n", p=128)[:, :, 128 * m:128 * m + 128],
                      writes=[b_wsc[l][ci]])
            for m in range(8):
                std("wo%d" % m, w_out[l], 128 * m, 8, 128)

        wstate = {"i": 0}

        def wload(l, name):
            ci = cidx[name]
            n = plan[ci][1]
            s = wstate["i"] % NW
            wstate["i"] += 1
            k.dma(sp, wring[s][:, 0:n], wsc[l, ci][:, 0:n], reads=[b_wsc[l][ci]], writes=[b_wring[s]], sem=wsem[s])
            return wring[s], b_wring[s]

        def rope_tables():
            invf = cst_t[0:32, 128:129]
            sgn = cst_t[0:32, 129:130]
            for t in range(n_tiles + 1):
                N = NMETA if t == 0 else TW
                t0 = 0 if t == 0 else NMETA + (t - 1) * TW
                if t == 0:
                    k.op(dve, lambda: nc.vector.tensor_copy(out=posf[:, 0:N], in_=cst_t[0:32, 132:148]),
                         reads=[b_cst], writes=[b_posf])
                else:
                    ld(sp, posi[:, 0:N], pos_in[0, (t - 1) * TW:t * TW].partition_broadcast(32), writes=[b_posi])
                    k.op(dve, lambda: nc.vector.tensor_copy(out=posf[:, 0:N], in_=posi[:, 0:N]),
                         reads=[b_posi], writes=[b_posf])
                    k.op(dve, lambda: nc.vector.tensor_scalar_add(out=posf[:, 0:N], in0=posf[:, 0:N], scalar1=float(NMETA)),
                         reads=[b_posf], writes=[b_posf])
                k.op(dve, lambda: nc.vector.tensor_scalar_mul(out=posf[:, 0:N], in0=posf[:, 0:N], scalar1=invf),
                     reads=[b_posf, b_cst], writes=[b_posf])
                C1 = 6.28125
                C2 = 2.0 * math.pi - C1
                for which, shift in ((0, math.pi * 0.5), (1, 0.0)):
                    a = ang[which]
                    ba = b_ang[which]
                    k.op(dve, lambda a=a, shift=shift: nc.vector.tensor_scalar_add(out=a[:, 0:N], in0=posf[:, 0:N], scalar1=shift),
                         reads=[b_posf], writes=[ba])
                    k.op(dve, lambda a=a: nc.vector.tensor_scalar_mul(out=nq[:, 0:N], in0=a[:, 0:N], scalar1=1.0 / (2.0 * math.pi)),
                         reads=[ba], writes=[b_nq])
                    k.op(dve, lambda: nc.vector.tensor_copy(out=posi[:, 0:N], in_=nq[:, 0:N]), reads=[b_nq], writes=[b_posi])
                    k.op(dve, lambda: nc.vector.tensor_copy(out=nq[:, 0:N], in_=posi[:, 0:N]), reads=[b_posi], writes=[b_nq])
                    k.op(dve, lambda a=a: nc.vector.scalar_tensor_tensor(out=a[:, 0:N], in0=nq[:, 0:N], scalar=-C1, in1=a[:, 0:N], op0=ALU.mult, op1=ALU.add),
                         reads=[b_nq, ba], writes=[ba])
                    k.op(dve, lambda a=a: nc.vector.scalar_tensor_tensor(out=a[:, 0:N], in0=nq[:, 0:N], scalar=-C2, in1=a[:, 0:N], op0=ALU.mult, op1=ALU.add),
                         reads=[b_nq, ba], writes=[ba])
                    k.op(dve, lambda a=a: nc.vector.tensor_single_scalar(out=nq[:, 0:N], in_=a[:, 0:N], scalar=math.pi, op=ALU.is_gt),
                         reads=[ba], writes=[b_nq])
                    k.op(dve, lambda a=a: nc.vector.scalar_tensor_tensor(out=a[:, 0:N], in0=nq[:, 0:N], scalar=-2.0 * math.pi, in1=a[:, 0:N], op0=ALU.mult, op1=ALU.add),
                         reads=[b_nq, ba], writes=[ba])
                    k.op(dve, lambda a=a: nc.vector.tensor_scalar_max(out=a[:, 0:N], in0=a[:, 0:N], scalar1=-3.1415925), reads=[ba], writes=[ba])
                    k.op(dve, lambda a=a: nc.vector.tensor_scalar_min(out=a[:, 0:N], in0=a[:, 0:N], scalar1=3.1415925), reads=[ba], writes=[ba])
                    if which == 0:
                        k.op(act, lambda a=a: nc.scalar.activation(out=a[:, 0:N], in_=a[:, 0:N], func=AF.Sin),
                             reads=[ba], writes=[ba])
                        k.dma(pool, cos_d[:, t0:t0 + N], a[:, 0:N], reads=[ba], writes=[b_csd[t]])
                    else:
                        k.op(act, lambda a=a: nc.scalar.activation(out=a[:, 0:N], in_=a[:, 0:N], func=AF.Sin, scale=sgn),
                             reads=[ba, b_cst], writes=[ba])
                        k.dma(pool, sin_d[:, t0:t0 + N], a[:, 0:N], reads=[ba], writes=[b_csd[t]])

        gring = Ring([0, 1, 2, 3, 4, 5, 6])

        def gbank():
            i = gring.next()
            return banks[i], b_banks[i]

        def proj(l, name, M, N, rhs_list, rhs_bufs, coloff=0, mtot=None, kc=None):
            w, bw = wload(l, name)
            kc = kc or len(rhs_list)
            mt_ = mtot or M
            wv = w[:, 0:kc * mt_].rearrange("p (c m) -> p c m", m=mt_)
            bank, bb = gbank()
            for c in range(kc):
                k.op(pe, lambda c=c: nc.tensor.matmul(bank[0:M, 0:N], lhsT=wv[:, c, coloff:coloff + M], rhs=rhs_list[c],
                                                       start=(c == 0), stop=(c == kc - 1)),
                     reads=[bw, rhs_bufs[c]], writes=[bb])
            return bank, bb, wv, bw

        def rstd_from(bank, bb, P, N, inv_n, dst, bdst):
            k.op(act, lambda: nc.scalar.activation(out=dst[0:P, 0:N], in_=bank[0:P, 0:N], func=AF.Ln, bias=cst_t[0:P, 148:149], scale=inv_n),
                 reads=[bb, b_cst], writes=[bdst])
            k.op(act, lambda: nc.scalar.activation(out=dst[0:P, 0:N], in_=dst[0:P, 0:N], func=AF.Exp, scale=-0.5), reads=[bdst], writes=[bdst])

        sq_ring = Ring([0, 1])
        sqk_ring = Ring([0, 1])
        rs_ring = Ring([0, 1, 2])
        raw_ring = Ring([0, 1])

        def emit_loads(l, t, xb):
            N = NMETA if t == 0 else TW
            x_t, bx = xt[xb], b_xt[xb]
            if l == 0:
                src = metaT.rearrange("(c p) n -> p c n", p=128) if t == 0 else \
                    xT.rearrange("(c p) n -> p c n", p=128)[:, :, (t - 1) * TW:t * TW]
                ld(sp, x_t[:, :, 0:N], src, writes=bx)
            else:
                if t == 0:
                    ld(sp, x_t[:, :, 0:N], xs_meta[:, :, :], reads=[b_xs_meta], writes=bx)
                else:
                    ld(sp, x_t[:, :, 0:N], xs[t - 1], reads=[b_xs[t - 1]], writes=bx)
            t0 = 0 if t == 0 else NMETA + (t - 1) * TW
            ld(sp, cs_t2[xb][64:96, 0, 0:N], cos_d[:, t0:t0 + N], reads=[b_csd[t]], writes=[b_cs2[xb]])
            ld(sp, cs_t2[xb][64:96, 1, 0:N], sin_d[:, t0:t0 + N], reads=[b_csd[t]], writes=[b_cs2[xb]])

        def tile_layer(l, t, xb):
            N = NMETA if t == 0 else TW
            NB = 1 if t == 0 else 4
            last = (l == n_layers - 1)
            x_t, bx = xt[xb], b_xt[xb]
            cs_t, b_cs = cs_t2[xb], b_cs2[xb]
            t0 = 0 if t == 0 else NMETA + (t - 1) * TW
            for j, (gt, col, tab) in enumerate(((gq_t, 0, 0), (gq_t, 1, 1), (gk_t, 0, 0), (gk_t, 1, 1))):
                k.op(dve, lambda j=j, gt=gt, col=col, tab=tab: nc.vector.tensor_scalar_mul(
                    out=cg_t[64:96, j, 0:N], in0=cs_t[64:96, tab, 0:N], scalar1=gt[64:96, l, col:col + 1]),
                    reads=[b_cs, b_gains], writes=[b_cg])

            bank, bb = gbank()
            for c in range(8):
                i = sq_ring.next()
                k.op(act, lambda c=c, i=i: nc.scalar.activation(out=sq[i][:, 0:N], in_=x_t[:, c, 0:N], func=AF.Square),
                     reads=[bx[c]], writes=[b_sq[i]])
                k.op(pe, lambda c=c, i=i: nc.tensor.matmul(bank[:, 0:N], lhsT=ones_bf[:, :], rhs=sq[i][:, 0:N], start=(c == 0), stop=(c == 7)),
                     reads=[b_ones, b_sq[i]], writes=[bb])
            r = rs_ring.next()
            rstd_from(bank, bb, 128, N, 1.0 / D, rs[r], b_rs[r])
            for c in range(8):
                k.op(dve, lambda c=c: nc.vector.scalar_tensor_tensor(out=hT[:, c, 0:N], in0=x_t[:, c, 0:N], scalar=gn_t[:, l, c:c + 1],
                                                                     in1=rs[r][:, 0:N], op0=ALU.mult, op1=ALU.mult),
                     reads=[bx[c], b_rs[r], b_gains], writes=[b_hT[c]])
            h_rhs = [hT[:, c, 0:N] for c in range(8)]

            if stage < 3.1005:
                return None
            ssb, bssb = banks[7], b_banks[7]
            if stage < 3.1015:
                wload(l, "ckv0")
                return None
            if stage < 3.1025:
                proj(l, "ckv0", 128, N, h_rhs, b_hT)
                return None
            for j in range(2):
                bank, bb, _, _ = proj(l, "ckv%d" % j, 128, N, h_rhs, b_hT)
                k.op(dve, lambda j=j, bank=bank: nc.vector.tensor_copy(out=ckv_raw[:, j, 0:N], in_=bank[:, 0:N]),
                     reads=[bb], writes=[b_ckv_raw[j]])
                if stage < 3.1035:
                    if j == 1:
                        return None
                    continue
                i = sq_ring.next()
                k.op(act, lambda i=i, j=j: nc.scalar.activation(out=sq[i][:, 0:N], in_=ckv_raw[:, j, 0:N], func=AF.Square),
                     reads=[b_ckv_raw[j]], writes=[b_sq[i]])
                k.op(pe, lambda j=j, i=i: nc.tensor.matmul(ssb[:, 0:N], lhsT=ones_bf[:, :], rhs=sq[i][:, 0:N], start=(j == 0), stop=(j == 1)),
                     reads=[b_ones, b_sq[i]], writes=[bssb])
            if stage < 3.1045:
                return None
            r = rs_ring.next()
            rstd_from(ssb, bssb, 128, N, 1.0 / 256, rs[r], b_rs[r])
            for j in range(2):
                k.op(dve, lambda j=j: nc.vector.scalar_tensor_tensor(out=ckvn[:, j, 0:N], in0=ckv_raw[:, j, 0:N], scalar=gkva_t[:, l, j:j + 1],
                                                                     in1=rs[r][:, 0:N], op0=ALU.mult, op1=ALU.mult),
                     reads=[b_ckv_raw[j], b_rs[r], b_gains], writes=[b_ckvn[j]])
            if stage < 3.1105:
                return None
            w, bw = wload(l, "kr")
            wv = w[:, 0:1536].rearrange("p (c m) -> p c m", m=192)
            for v_ in range(2):
                bank, bb = gbank()
                for c in range(8):
                    k.op(pe, lambda c=c, v_=v_, bank=bank: nc.tensor.matmul(bank[0:96, 0:N], lhsT=wv[:, c, 96 * v_:96 * v_ + 96], rhs=h_rhs[c],
                                                                         start=(c == 0), stop=(c == 7)),
                         reads=[bw, b_hT[c]], writes=[bb])
                k.op(dve, lambda v_=v_, bank=bank: nc.vector.tensor_copy(out=kr_raw[64:96, v_, 0:N], in_=bank[64:96, 0:N]),
                     reads=[bb], writes=[b_kr_raw])
                if v_ == 0:
                    for i in range(2):
                        k.op(act, lambda i=i: nc.scalar.activation(out=sqk[i][64:96, 0:N], in_=kr_raw[64:96, 0, 0:N], func=AF.Square),
                             reads=[b_kr_raw], writes=[b_sqk_rope[i]])
            k.op(dve, lambda: nc.vector.tensor_tensor(out=tmp32[0][64:96, 0:N], in0=kr_raw[64:96, 0, 0:N], in1=cg_t[64:96, 2, 0:N], op=ALU.mult),
                 reads=[b_kr_raw, b_cg], writes=[b_tmp32[0]])
            k.op(dve, lambda: nc.vector.tensor_tensor(out=kR[64:96, 0:N], in0=kr_raw[64:96, 1, 0:N], in1=cg_t[64:96, 3, 0:N], op=ALU.mult),
                 reads=[b_kr_raw, b_cg], writes=[b_kR])
            k.op(dve, lambda: nc.vector.tensor_tensor(out=kR[64:96, 0:N], in0=kR[64:96, 0:N], in1=tmp32[0][64:96, 0:N], op=ALU.add),
                 reads=[b_kR, b_tmp32[0]], writes=[b_kR])
            if stage < 3.1205:
                return None
            w, bw = wload(l, "kvbk")
            wkk = w[:, 0:2 * NH * 64].rearrange("p (c h m) -> p c h m", h=NH, m=64)
            ckv_rhs = [ckvn[:, c, 0:N] for c in range(2)]

            def ktv(rows, h):
                if t == 0:
                    return kT_t[rows, 0, h, 0:N]
                return kT_t[rows, :, h, :]

            def as_blk(ap):
                return ap if t == 0 else ap.rearrange("p (b c) -> p b c", c=128)

            kpend = None
            for h in range(NH):
                bank, bb = gbank()
                for c in range(2):
                    k.op(pe, lambda c=c, h=h, bank=bank: nc.tensor.matmul(bank[0:64, 0:N], lhsT=wkk[:, c, h, :], rhs=ckv_rhs[c], start=(c == 0), stop=(c == 1)),
                         reads=[bw, b_ckvn[c]], writes=[bb])
                ri_ = raw_ring.next()
                k.op(dve, lambda ri_=ri_, bank=bank: nc.vector.tensor_copy(out=rec[ri_][0:64, 0:N], in_=bank[0:64, 0:N]), reads=[bb], writes=[b_rec[ri_]])
                i = sqk_ring.next()
                k.op(act, lambda i=i, ri_=ri_: nc.scalar.activation(out=sqk[i][0:64, 0:N], in_=rec[ri_][0:64, 0:N], func=AF.Square),
                     reads=[b_rec[ri_]], writes=[b_sqk[i]])

                def ktail(h=h, ri_=ri_, i=i):
                    bank2, bb2 = gbank()
                    k.op(pe, lambda: nc.tensor.matmul(bank2[0:96, 0:N], lhsT=ones_bf[0:96, 0:96], rhs=sqk[i][0:96, 0:N], start=True, stop=True),
                         reads=[b_ones, b_sqk[i], b_sqk_rope[i]], writes=[bb2])
                    r = rs_ring.next()
                    rstd_from(bank2, bb2, 96, N, 1.0 / 96, rs[r], b_rs[r])
                    k.op(dve, lambda: nc.vector.scalar_tensor_tensor(
                        out=ktv(slice(0, 64), h), in0=as_blk(rec[ri_][0:64, 0:N]), scalar=gk_t[0:64, l, 0:1],
                        in1=as_blk(rs[r][0:64, 0:N]), op0=ALU.mult, op1=ALU.mult),
                        reads=[b_rec[ri_], b_rs[r], b_gains], writes=[b_kT])
                    k.op(pool, lambda: nc.gpsimd.tensor_tensor(out=ktv(slice(64, 96), h), in0=as_blk(kR[64:96, 0:N]), in1=as_blk(rs[r][64:96, 0:N]), op=ALU.mult),
                         reads=[b_kR, b_rs[r]], writes=[b_kT])
                if kpend is not None:
                    kpend()
                kpend = ktail
            kpend()
            if stage < 3.1305:
                return None
            w, bw = wload(l, "kvbv")
            wvv = w[:, 0:1024].rearrange("p (c n) -> p c n", n=512)
            for tb in range(NB):
                nt = N if t == 0 else 128
                bank, bb = gbank()
                for c in range(2):
                    k.op(pe, lambda c=c, tb=tb, bank=bank: nc.tensor.matmul(bank[0:nt, 0:512], lhsT=ckvn[:, c, tb * 128:tb * 128 + nt], rhs=wvv[:, c, :],
                                                                         start=(c == 0), stop=(c == 1)),
                         reads=[bw, b_ckvn[c]], writes=[bb])
                pv = bank[0:nt, 0:512].rearrange("p (h two d) -> p h two d", two=2, d=64)
                vv = va_t[0:nt, tb].rearrange("p (h two) c -> p h two c", two=2)
                k.op(act, lambda pv=pv, vv=vv: nc.scalar.copy(out=vv[:, :, 0, 0:64], in_=pv[:, :, 0, :]), reads=[bb, b_va_ones], writes=[b_va])
                k.op(dve, lambda pv=pv, vv=vv: nc.vector.tensor_copy(out=vv[:, :, 1, 64:128], in_=pv[:, :, 1, :]), reads=[bb, b_va_ones], writes=[b_va])
            if stage < 3.1405:
                return None
            n_st = 4 if stage >= 3.2 else int(round((stage - 3.14) * 1000))
            if t == 0:
                if n_st >= 1:
                    k.dma(pool, kT_d[0].rearrange("p h c -> p (h c)"), kT_t[:, 0].rearrange("p h c -> p (h c)"), reads=[b_kT], writes=[b_kvd[0]])
                if n_st >= 2:
                    k.dma(pool, v_d[0][0:N].rearrange("p h c -> p (h c)"), va_t[0:N, 0].rearrange("p h c -> p (h c)"), reads=[b_va], writes=[b_kvd[0]])
            else:
                kb0 = 1 + 4 * (t - 1)
                if n_st >= 3:
                    k.dma(pool, kT_d[kb0:kb0 + 4].rearrange("b p h c -> p b (h c)"), kT_t[:].rearrange("p b h c -> p b (h c)"),
                          reads=[b_kT], writes=[b_kvd[t]])
                if n_st >= 4:
                    k.dma(pool, v_d[kb0:kb0 + 4].rearrange("b p h c -> p b (h c)"), va_t[:].rearrange("p b h c -> p b (h c)"),
                          reads=[b_va], writes=[b_kvd[t]])

            if stage < 3.25:
                return None
            for j in range(6):
                bank, bb, _, _ = proj(l, "cq%d" % j, 128, N, h_rhs, b_hT)
                k.op(dve, lambda j=j, bank=bank: nc.vector.tensor_copy(out=cq_raw[:, j, 0:N], in_=bank[:, 0:N]), reads=[bb], writes=[b_cq_raw[j]])
                i = sq_ring.next()
                k.op(act, lambda i=i, j=j: nc.scalar.activation(out=sq[i][:, 0:N], in_=cq_raw[:, j, 0:N], func=AF.Square), reads=[b_cq_raw[j]], writes=[b_sq[i]])
                k.op(pe, lambda j=j, i=i: nc.tensor.matmul(ssb[:, 0:N], lhsT=ones_bf[:, :], rhs=sq[i][:, 0:N], start=(j == 0), stop=(j == 5)),
                     reads=[b_ones, b_sq[i]], writes=[bssb])
            r = rs_ring.next()
            rstd_from(ssb, bssb, 128, N, 1.0 / 768, rs[r], b_rs[r])
            for j in range(6):
                k.op(dve, lambda j=j: nc.vector.scalar_tensor_tensor(out=cqn[:, j, 0:N], in0=cq_raw[:, j, 0:N], scalar=gqa_t[:, l, j:j + 1],
                                                                     in1=rs[r][:, 0:N], op0=ALU.mult, op1=ALU.mult),
                     reads=[b_cq_raw[j], b_rs[r], b_gains], writes=[b_cqn[j]])
            cq_rhs = [cqn[:, c, 0:N] for c in range(6)]
            qpend = None
            for h in range(NH):
                w, bw = wload(l, "qb%d" % h)
                wq_ = w[:, 0:1152].rearrange("p (c m) -> p c m", m=192)
                bq, bbq = gbank()
                for c in range(6):
                    k.op(pe, lambda c=c, bq=bq: nc.tensor.matmul(bq[0:96, 0:N], lhsT=wq_[:, c, 0:96], rhs=cq_rhs[c], start=(c == 0), stop=(c == 5)),
                         reads=[bw, b_cqn[c]], writes=[bbq])
                bs_, bbs = gbank()
                for c in range(6):
                    k.op(pe, lambda c=c, bs_=bs_: nc.tensor.matmul(bs_[0:96, 0:N], lhsT=wq_[:, c, 96:192], rhs=cq_rhs[c], start=(c == 0), stop=(c == 5)),
                         reads=[bw, b_cqn[c]], writes=[bbs])
                ri_ = raw_ring.next()
                k.op(dve, lambda ri_=ri_, bq=bq: nc.vector.tensor_copy(out=rec[ri_][0:96, 0:N], in_=bq[0:96, 0:N]), reads=[bbq], writes=[b_rec[ri_]])
                i = sqk_ring.next()
                k.op(act, lambda i=i, ri_=ri_: nc.scalar.activation(out=sqk[i][0:96, 0:N], in_=rec[ri_][0:96, 0:N], func=AF.Square),
                     reads=[b_rec[ri_]], writes=[b_sqk[i], b_sqk_rope[i]])
                def qtail(h=h, i=i, ri_=ri_, bq=bq, bbq=bbq, bs_=bs_, bbs=bbs):
                    b3, bb3 = gbank()
                    k.op(pe, lambda i=i, b3=b3: nc.tensor.matmul(b3[0:96, 0:N], lhsT=ones_bf[0:96, 0:96], rhs=sqk[i][0:96, 0:N], start=True, stop=True),
                         reads=[b_ones, b_sqk[i], b_sqk_rope[i]], writes=[bb3])
                    r = rs_ring.next()
                    rstd_from(b3, bb3, 96, N, 1.0 / 96, rs[r], b_rs[r])
                    k.op(dve, lambda h=h, ri_=ri_, r=r: nc.vector.scalar_tensor_tensor(out=qT[0:64, h, 0:N], in0=rec[ri_][0:64, 0:N], scalar=gq_t[0:64, l, 0:1],
                                                                                    in1=rs[r][0:64, 0:N], op0=ALU.mult, op1=ALU.mult),
                         reads=[b_rec[ri_], b_rs[r], b_gains], writes=[b_qT[h]])
                    k.op(pool, lambda ri_=ri_: nc.gpsimd.tensor_tensor(out=tmp32[0][64:96, 0:N], in0=rec[ri_][64:96, 0:N], in1=cg_t[64:96, 0, 0:N], op=ALU.mult),
                         reads=[b_rec[ri_], b_cg], writes=[b_tmp32[0]])
                    k.op(dve, lambda bs_=bs_: nc.vector.tensor_tensor(out=tmp32[1][64:96, 0:N], in0=bs_[64:96, 0:N], in1=cg_t[64:96, 1, 0:N], op=ALU.mult),
                         reads=[bbs, b_cg], writes=[b_tmp32[1]])
                    k.op(pool, lambda: nc.gpsimd.tensor_tensor(out=tmp32[0][64:96, 0:N], in0=tmp32[0][64:96, 0:N], in1=tmp32[1][64:96, 0:N], op=ALU.add),
                         reads=[b_tmp32[0], b_tmp32[1]], writes=[b_tmp32[0]])
                    k.op(pool, lambda h=h, r=r: nc.gpsimd.tensor_tensor(out=qT[64:96, h, 0:N], in0=tmp32[0][64:96, 0:N], in1=rs[r][64:96, 0:N], op=ALU.mult),
                         reads=[b_tmp32[0], b_rs[r]], writes=[b_qT[h]])
                if qpend is not None:
                    qpend()
                qpend = qtail
            qpend()
            for c in range(4):
                bank, bb, _, _ = proj(l, "zm%d" % c, 128, N, h_rhs, b_hT)
                k.op(act, lambda c=c, bank=bank: nc.scalar.activation(out=zm[:, c, 0:N], in_=bank[:, 0:N], func=AF.Silu), reads=[bb], writes=[b_zm[c]])

            if stage < 3.35:
                return None
            if pending_loads[0] is not None:
                pending_loads[0]()
                pending_loads[0] = None
            scale = 1.0 / math.sqrt(96.0)
            kbs = [(0, NMETA, 0, t == 0)]
            if t > 0:
                kbs += [(1 + j, 128, 0, False) for j in range(4 * (t - 1))]
                kbs += [(1 + 4 * (t - 1) + r_, 128, 128 * r_, True) for r_ in range(4)]
            sring = Ring([0, 1, 2])
            pring = Ring([0, 1, 2, 3])
            for hp in range(2):
                pv_pend = []
                for bi, (kb, nk, c0, diag) in enumerate(kbs):
                    s = wstate.setdefault("kv", 0) % NKV
                    wstate["kv"] += 1
                    tk = 0 if kb == 0 else 1 + (kb - 1) // 4
                    k.dma(sp, kslot[s][:, :, :], kT_d[kb][:, 4 * hp:4 * hp + 4, :], reads=[b_kvd[tk]], writes=[b_kslot[s]], sem=ksem[s])
                    k.dma(sp, vslot[s][0:nk], v_d[kb][0:nk, 4 * hp:4 * hp + 4, :], reads=[b_kvd[tk]], writes=[b_vslot[s]], sem=vsem[s])
                    for hh in range(4):
                        h = 4 * hp + hh
                        si = sring.next()
                        sbk, bsb = banks[si], b_banks[si]
                        k.op(pe, lambda s=s, hh=hh, h=h, sbk=sbk: nc.tensor.matmul(sbk[0:nk, c0:N], lhsT=kslot[s][:, hh, 0:nk], rhs=qT[:, h, c0:N], start=True, stop=True),
                             reads=[b_kslot[s], b_qT[h]], writes=[bsb])
                        pi = pring.next()
                        k.op(act, lambda pi=pi, sbk=sbk: nc.scalar.activation(out=pbuf[pi][0:nk, c0:N], in_=sbk[0:nk, c0:N], func=AF.Exp, scale=scale),
                             reads=[bsb], writes=[b_pbuf[pi]])
                        if diag:
                            wd = min(128, N - c0)
                            k.op(pool, lambda pi=pi, wd=wd: nc.gpsimd.tensor_tensor(out=pbuf[pi][0:nk, c0:c0 + wd], in0=pbuf[pi][0:nk, c0:c0 + wd],
                                                                                    in1=tri_bf[0:nk, 0:wd], op=ALU.mult),
                                 reads=[b_pbuf[pi], b_tri], writes=[b_pbuf[pi]])
                        ob, bob = banks[3 + hh], b_banks[3 + hh]

                        def pv(s=s, hh=hh, pi=pi, ob=ob, bob=bob, nk=nk, c0=c0, bi=bi):
                            k.op(pe, lambda: nc.tensor.matmul(ob[:, c0:N], lhsT=vslot[s][0:nk, hh, :], rhs=pbuf[pi][0:nk, c0:N],
                                                              start=(bi == 0), stop=(bi == len(kbs) - 1)),
                                 reads=[b_vslot[s], b_pbuf[pi]], writes=[bob])
                        pv_pend.append(pv)
                        if len(pv_pend) > 2:
                            pv_pend.pop(0)()
                while pv_pend:
                    pv_pend.pop(0)()
                for hh in range(4):
                    h = 4 * hp + hh
                    ob, bob = banks[3 + hh], b_banks[3 + hh]
                    pr, par = h // 2, h % 2
                    orow = slice(0, 64) if par == 0 else slice(64, 128)
                    drow = slice(64, 128) if par == 0 else slice(0, 64)
                    ri = hh % 2
                    k.op(act, lambda ob=ob, ri=ri: nc.scalar.activation(out=rec[ri][drow, 0:N], in_=ob[drow, 0:N], func=AF.Ln), reads=[bob], writes=[b_rec[ri]])
                    k.op(act, lambda ri=ri: nc.scalar.activation(out=rec[ri][orow, 0:N], in_=rec[ri][drow, 0:N], func=AF.Exp, scale=-1.0),
                         reads=[b_rec[ri]], writes=[b_rec[ri]])
                    k.op(dve, lambda ob=ob, ri=ri: nc.vector.tensor_tensor(out=otmp[ri][orow, 0:N], in0=ob[orow, 0:N], in1=rec[ri][orow, 0:N], op=ALU.mult),
                         reads=[bob, b_rec[ri]], writes=[b_otmp[ri]])
                    k.op(pool, lambda ri=ri, pr=pr: nc.gpsimd.tensor_tensor(out=om[orow, pr, 0:N], in0=otmp[ri][orow, 0:N], in1=zm[orow, pr, 0:N], op=ALU.mult),
                         reads=[b_otmp[ri], b_zm[pr]], writes=[b_om[pr]])

            if stage < 3.45:
                return None
            for g in range(4):
                bank, bb, _, _ = proj(l, "u%d" % g, 128, N, h_rhs, b_hT)
                k.op(dve, lambda g=g, bank=bank: nc.vector.tensor_copy(out=u_t[:, g, 16:16 + N], in_=bank[:, 0:N]), reads=[bb], writes=[b_u[g]])
            for g in range(4):
                bank, bb, _, _ = proj(l, "zp%d" % g, 128, N, h_rhs, b_hT)
                k.op(act, lambda g=g, bank=bank: nc.scalar.activation(out=zp[:, g, 0:N], in_=bank[:, 0:N], func=AF.Silu), reads=[bb], writes=[b_zp[g]])
            wpgt, bwpg = wload(l, "pg")
            wpgv = wpgt[:, 0:512].rearrange("p (g m) -> p g m", m=128)
            for g in range(4):
                W_ = 16 + N
                src, bsrc = u_t[:, g, 0:W_], [b_u[g], b_uh[g]]
                sh = 1
                pi_ = 0
                while sh < POOL_W[g]:
                    dst, bdst = pt[pi_], b_pt[pi_]
                    k.op(pool, lambda src=src, dst=dst, sh=sh: nc.gpsimd.tensor_tensor(out=dst[:, sh:W_], in0=src[:, sh:W_], in1=src[:, 0:W_ - sh], op=ALU.add),
                         reads=bsrc, writes=[bdst])
                    src, bsrc = dst[:, 0:W_], [bdst]
                    sh *= 2
                    pi_ ^= 1
                if t == 0:
                    k.op(dve, lambda src=src, g=g, pi_=pi_: nc.vector.tensor_tensor(out=pt[pi_][:, 16:16 + N], in0=src[:, 16:16 + N], in1=cst2_t[:, g, :], op=ALU.mult),
                         reads=bsrc + [b_cst], writes=[b_pt[pi_]])
                    k.op(dve, lambda g=g, pi_=pi_: nc.vector.tensor_tensor(out=mixed[g][:, 0:N], in0=pt[pi_][:, 16:16 + N], in1=u_t[:, g, 16:16 + N], op=ALU.subtract),
                         reads=[b_pt[pi_], b_u[g]], writes=[b_mixed[g]])
                else:
                    k.op(dve, lambda src=src, g=g: nc.vector.scalar_tensor_tensor(out=mixed[g][:, 0:N], in0=src[:, 16:16 + N], scalar=1.0 / POOL_W[g],
                                                                                 in1=u_t[:, g, 16:16 + N], op0=ALU.mult, op1=ALU.subtract),
                         reads=bsrc + [b_u[g]], writes=[b_mixed[g]])
                k.op(pool, lambda g=g: nc.gpsimd.tensor_copy(out=u_t[:, g, 0:16], in_=u_t[:, g, N:N + 16]), reads=[b_u[g]], writes=[b_uh[g]])
                bank, bb = gbank()
                k.op(pe, lambda g=g, bank=bank: nc.tensor.matmul(bank[:, 0:N], lhsT=wpgv[:, g, :], rhs=mixed[g][:, 0:N], start=True, stop=True),
                     reads=[bwpg, b_mixed[g]], writes=[bb])
                k.op(dve, lambda g=g, bank=bank: nc.vector.scalar_tensor_tensor(out=pg[:, g, 0:N], in0=bank[:, 0:N], scalar=psc_t[:, l, g:g + 1],
                                                                               in1=zp[:, g, 0:N], op0=ALU.mult, op1=ALU.mult),
                     reads=[bb, b_zp[g], b_gains], writes=[b_pg[g]])

            if stage < 3.55:
                return None
            if last and t == 0 and not meta_out:
                return

            pg_rhs = [pg[:, c, 0:N] for c in range(4)]
            om_rhs = [om[:, c, 0:N] for c in range(4)]
            sgr = Ring([0, 1, 2, 3])
            mtr = Ring([0, 1])
            for m in range(8):
                b1, bb1, _, _ = proj(l, "gp%d" % m, 128, N, h_rhs, b_hT)
                s1 = sgr.next()
                k.op(act, lambda b1=b1, s1=s1: nc.scalar.activation(out=sg[s1][:, 0:N], in_=b1[:, 0:N], func=AF.Sigmoid), reads=[bb1], writes=[b_sg[s1]])
                b2, bb2, _, _ = proj(l, "gm%d" % m, 128, N, h_rhs, b_hT)
                s2 = sgr.next()
                k.op(act, lambda b2=b2, s2=s2: nc.scalar.activation(out=sg[s2][:, 0:N], in_=b2[:, 0:N], func=AF.Sigmoid), reads=[bb2], writes=[b_sg[s2]])
                w, bw = wload(l, "up%d" % m)
                wu = w[:, 0:1024].rearrange("p (t c m) -> p t c m", t=2, m=128)
                b3, bb3 = gbank()
                for c in range(4):
                    k.op(pe, lambda c=c, b3=b3: nc.tensor.matmul(b3[:, 0:N], lhsT=wu[:, 0, c, :], rhs=pg_rhs[c], start=(c == 0), stop=(c == 3)),
                         reads=[bw, b_pg[c]], writes=[bb3])
                m1 = mtr.next()
                k.op(dve, lambda b3=b3, s1=s1, m1=m1: nc.vector.tensor_tensor(out=mt[m1][:, 0:N], in0=b3[:, 0:N], in1=sg[s1][:, 0:N], op=ALU.mult),
                     reads=[bb3, b_sg[s1]], writes=[b_mt[m1]])
                b4, bb4 = gbank()
                for c in range(4):
                    k.op(pe, lambda c=c, b4=b4: nc.tensor.matmul(b4[:, 0:N], lhsT=wu[:, 1, c, :], rhs=om_rhs[c], start=(c == 0), stop=(c == 3)),
                         reads=[bw, b_om[c]], writes=[bb4])
                m2 = mtr.next()
                k.op(dve, lambda b4=b4, s2=s2, m2=m2: nc.vector.tensor_tensor(out=mt[m2][:, 0:N], in0=b4[:, 0:N], in1=sg[s2][:, 0:N], op=ALU.mult),
                     reads=[bb4, b_sg[s2]], writes=[b_mt[m2]])
                k.op(pool, lambda m=m, m1=m1, m2=m2: nc.gpsimd.tensor_tensor(out=merged[:, m, 0:N], in0=mt[m1][:, 0:N], in1=mt[m2][:, 0:N], op=ALU.add),
                     reads=[b_mt[m1], b_mt[m2]], writes=[b_merged[m]])

            if stage < 3.65:
                return None
            mg_rhs = [merged[:, c, 0:N] for c in range(8)]
            for m in range(8):
                bank, bb, _, _ = proj(l, "wo%d" % m, 128, N, mg_rhs, b_merged)
                k.op(dve, lambda m=m, bank=bank: nc.vector.tensor_tensor(out=x_t[:, m, 0:N], in0=x_t[:, m, 0:N], in1=bank[:, 0:N], op=ALU.add),
                     reads=[bb, bx[m]], writes=[bx[m]])
            if last and t == 0:
                return k.dma(pool, metaT_out.rearrange("(c p) n -> p c n", p=128), x_t[:, :, 0:N], reads=bx, writes=[b_xs_meta])
            if last:
                return k.dma(pool, yT.rearrange("(c p) n -> p c n", p=128)[:, :, (t - 1) * TW:t * TW], x_t[:, :, 0:N], reads=bx, writes=[b_xs[t - 1]])
            if t == 0:
                k.dma(pool, xs_meta[:, :, :], x_t[:, :, 0:N], reads=bx, writes=[b_xs_meta])
            else:
                k.dma(pool, xs[t - 1], x_t[:, :, 0:N], reads=bx, writes=[b_xs[t - 1]])
            return None

        cst2 = din("cst2", [128, 4, NMETA])
        cst2_t = sb("cst2_t", [128, 4, NMETA], F32)
        ld(sp, cst2_t[:], cst2[:, :, :], writes=[b_cst])

        if stage >= 1:
            emit_casts(0)
        if stage >= 2:
            rope_tables()
        out_evs = []
        pending_loads = [None]
        for l in range(n_layers if stage >= 3 else 0):
            if l > 0:
                k.epoch()
            if l + 1 < n_layers:
                emit_casts(l + 1)
            for t in range(n_tiles + 1):
                gi = l * (n_tiles + 1) + t
                if gi == 0:
                    emit_loads(0, 0, 0)
                nxt = (l, t + 1) if t < n_tiles else ((l + 1, 0) if l + 1 < n_layers else None)
                pending_loads[0] = (lambda nxt=nxt, gi=gi: emit_loads(nxt[0], nxt[1], (gi + 1) % 2)) if nxt else None
                ev = tile_layer(l, t, gi % 2)
                if pending_loads[0] is not None:
                    pending_loads[0]()
                    pending_loads[0] = None
                if ev is not None:
                    out_evs.append(ev)
        for ev in out_evs:
            pool.wait(ev)
        for sm in k.all_sems:
            if sm.cnt:
                pool.wait((sm, sm.cnt))
    return nc


def make_consts():
    cst = np.zeros((128, 160), np.float32)
    p = np.arange(128)[:, None]
    c = np.arange(128)[None, :]
    cst[:, 0:128] = (p <= c).astype(np.float32)
    i = np.arange(32) % 16
    cst[0:32, 128] = (10000.0 ** (-(i.astype(np.float32)) / 16.0)).astype(np.float32)
    cst[0:32, 129] = np.where(np.arange(32) < 16, -1.0, 1.0)
    cst[0:32, 130] = np.where(np.arange(32) < 16, math.pi, -math.pi)
    cst[0:32, 131] = -math.pi
    cst[:, 132:148] = np.arange(16, dtype=np.float32)[None, :]
    cst[:, 148] = EPS
    cst2 = np.zeros((128, 4, NMETA), np.float32)
    for g, w in enumerate(POOL_W):
        cst2[:, g, :] = 1.0 / np.minimum(np.arange(1, NMETA + 1), w).astype(np.float32)[None, :]
    return cst, cst2


_CACHE = {}


def run(inputs, n_layers=DEPTH, n_tiles=SEQ // TW, stage=3.7):
    key = (n_layers, n_tiles, stage)
    if key not in _CACHE:
        _CACHE[key] = build(n_layers, n_tiles, stage)
    nc = _CACHE[key]
    S = n_tiles * TW
    cst, cst2 = make_consts()
    x = np.asarray(inputs["x"], np.float32)
    B = x.shape[0]
    shared = {
        "metaT": np.ascontiguousarray(np.asarray(inputs["meta_tokens"], np.float32).T),
        "cst": cst, "cst2": cst2,
    }
    for n in ("norm_gain", "w_in", "pool_w_group", "pool_scale", "pool_w_up", "q_a_norm_gain", "kv_a_norm_gain",
              "w_q_b", "w_kv_b", "q_norm_gain", "k_norm_gain", "mla_w_up", "w_out"):
        shared[n] = np.ascontiguousarray(np.asarray(inputs[n], np.float32)[:n_layers])
    in_maps = []
    for c in range(8):
        b = c % B
        m = dict(shared)
        m["xT"] = np.ascontiguousarray(x[b, :S].T)
        m["pos"] = np.ascontiguousarray(np.asarray(inputs["positions"], np.int32)[b:b + 1, :S])
        in_maps.append(m)
    res = run_bass_kernel_spmd(nc, in_maps, core_ids=list(range(8)))
    out = np.stack([np.ascontiguousarray(res.results[b]["yT"].T) for b in range(B)], axis=0)
    return out.astype(np.float32)


def run_unfused(inputs):
    lpl = LAYERS_PER_LAUNCH
    key = ("unfused", lpl)
    if key not in _CACHE:
        _CACHE[key] = build(lpl, SEQ // TW, meta_out=True)
    nc = _CACHE[key]
    cst, cst2 = make_consts()
    x = np.asarray(inputs["x"], np.float32)
    B = x.shape[0]
    xT = [np.ascontiguousarray(x[b].T) for b in range(B)]
    metaT = [np.ascontiguousarray(np.asarray(inputs["meta_tokens"], np.float32).T) for _ in range(B)]
    pos = np.asarray(inputs["positions"], np.int32)
    names = ("norm_gain", "w_in", "pool_w_group", "pool_scale", "pool_w_up", "q_a_norm_gain", "kv_a_norm_gain",
             "w_q_b", "w_kv_b", "q_norm_gain", "k_norm_gain", "mla_w_up", "w_out")
    for l in range(0, DEPTH, lpl):
        shared = {"cst": cst, "cst2": cst2}
        for n in names:
            shared[n] = np.ascontiguousarray(np.asarray(inputs[n], np.float32)[l:l + lpl])
        in_maps = []
        for c in range(8):
            b = c % B
            m = dict(shared)
            m["xT"] = xT[b]
            m["metaT"] = metaT[b]
            m["pos"] = np.ascontiguousarray(pos[b:b + 1])
            in_maps.append(m)
        res = run_bass_kernel_spmd(nc, in_maps, core_ids=list(range(8)))
        xT = [np.ascontiguousarray(res.results[b]["yT"]) for b in range(B)]
        metaT = [np.ascontiguousarray(res.results[b]["metaT_out"]) for b in range(B)]
    return np.stack([np.ascontiguousarray(xT[b].T) for b in range(B)], axis=0).astype(np.float32)


FUSED = True
LAYERS_PER_LAUNCH = 2


def kernel(**inputs):
    return run(inputs) if FUSED else run_unfused(inputs)
```

```python
import math
from contextlib import ExitStack

import numpy as np
import concourse.bass as bass
import concourse.mybir as mybir
from concourse.bass_utils import run_bass_kernel_spmd

F32 = mybir.dt.float32
BF16 = mybir.dt.bfloat16
I32 = mybir.dt.int32
AF = mybir.ActivationFunctionType
ALU = mybir.AluOpType

D = 1024
DEPTH = 4
BATCH = 4
SEQ = 8192
NMETA = 16
NH = 8
D_IN = 4640
EPS = 1e-6
TW = 512
POOL_W = (2, 4, 8, 16)
WSLOT = 1536
C_U, C_ZP, C_CQ, C_CKV, C_KR, C_ZM, C_GP, C_GM = 0, 512, 1024, 1792, 2048, 2080, 2592, 3616


class Sem:
    def __init__(self, h, owner=None):
        self.h = h
        self.cnt = 0
        self.owner = owner


class Buf:
    __slots__ = ("w", "r", "name")

    def __init__(self, name=""):
        self.w = None
        self.r = {}
        self.name = name


class Eng:
    def __init__(self, k, h, name):
        self.k = k
        self.h = h
        self.name = name
        self.sem = None
        self.known = {}
        self.new_epoch()

    def new_epoch(self):
        self.sem = self.k.new_sem(self.name, owner=self.name)

    def needs(self, ev):
        if ev is None:
            return False
        s, v = ev
        if self.known.get(s, 0) >= v:
            return False
        if self.name == "pe" and s.owner == "pe":
            return False
        return True

    def wait(self, ev):
        if self.needs(ev):
            s, v = ev
            self.h.wait_ge(s.h, v)
            self.known[s] = v

    def pending(self, evs):
        need = {}
        for ev in evs:
            if self.needs(ev):
                s, v = ev
                if need.get(s, 0) < v:
                    need[s] = v
        for s, v in need.items():
            self.known[s] = v
        return list(need.items())

    def issue(self, fn, evs):
        pend = self.pending(evs)
        for s, v in pend[:-1]:
            self.h.wait_ge(s.h, v)
        ins = fn()
        if pend:
            s, v = pend[-1]
            ins._wait_ge(s.h, v)
        return ins


class K:
    def __init__(self, nc, stack):
        self.nc = nc
        self.stack = stack
        self.nsem = 0
        self.all_sems = []
        self.pe = Eng(self, nc.tensor, "pe")
        self.act = Eng(self, nc.scalar, "act")
        self.dve = Eng(self, nc.vector, "dve")
        self.pool = Eng(self, nc.gpsimd, "pool")
        self.sp = Eng(self, nc.sync, "sp")
        self.store_sems = [self.new_sem("st%d" % i) for i in range(8)]
        self.store_i = 0

    def new_sem(self, name, owner=None):
        self.nsem += 1
        sm = Sem(self.stack.enter_context(self.nc.semaphore("%s_%d" % (name, self.nsem))), owner)
        self.all_sems.append(sm)
        return sm

    def epoch(self):
        for e in (self.pe, self.act, self.dve, self.pool, self.sp):
            e.new_epoch()

    def sb(self, name, shape, dt):
        return self.stack.enter_context(self.nc.sbuf_tensor(name, list(shape), dt))

    def ps(self, name):
        return self.stack.enter_context(self.nc.psum_tensor(name, [128, 512], F32))

    @staticmethod
    def _deps(reads, writes):
        evs = []
        for b in reads:
            evs.append(b.w)
        for b in writes:
            evs.append(b.w)
            evs.extend(b.r.items())
        return evs

    @staticmethod
    def _update(ev, reads, writes):
        s, v = ev
        for b in reads:
            if b.r.get(s, 0) < v:
                b.r[s] = v
        for b in writes:
            b.w = ev
            b.r = {}

    def op(self, eng, fn, reads=(), writes=()):
        ins = eng.issue(fn, self._deps(reads, writes))
        eng.sem.cnt += 1
        ins.then_inc(eng.sem.h, 1)
        ev = (eng.sem, eng.sem.cnt)
        self._update(ev, reads, writes)
        return ev

    def dma(self, q, out, in_, reads=(), writes=(), sem=None, extra=(), **kw):
        evs = self._deps(reads, writes) + list(extra)
        if sem is None:
            sem = self.store_sems[self.store_i % len(self.store_sems)]
            self.store_i += 1
            if sem.cnt:
                evs.append((sem, sem.cnt))
        ins = q.issue(lambda: q.h.dma_start(out=out, in_=in_, **kw), evs)
        sem.cnt += 16
        ins.then_inc(sem.h, 16)
        ev = (sem, sem.cnt)
        self._update(ev, reads, writes)
        return ev


class Ring:
    def __init__(self, items):
        self.items = items
        self.i = 0

    def next(self):
        it = self.items[self.i % len(self.items)]
        self.i += 1
        return it


def chunk_plan():
    p = []
    p += [("ckv0", 1024), ("ckv1", 1024), ("kr", 1536), ("kvbk", 1024), ("kvbv", 1024)]
    p += [("cq%d" % j, 1024) for j in range(6)]
    p += [("qb%d" % h, 1152) for h in range(NH)]
    p += [("zm%d" % c, 1024) for c in range(4)]
    p += [("u%d" % g, 1024) for g in range(4)]
    p += [("zp%d" % g, 1024) for g in range(4)]
    p += [("pg", 512)]
    for m in range(8):
        p += [("gp%d" % m, 1024), ("gm%d" % m, 1024), ("up%d" % m, 1024)]
    p += [("wo%d" % m, 1024) for m in range(8)]
    return p


def build(n_layers=DEPTH, n_tiles=SEQ // TW, stage=3.7, meta_out=False):
    nc = bass.Bass("TRN2", target_bir_lowering=False)
    S = n_tiles * TW
    L = NMETA + S
    NKB = 1 + 4 * n_tiles
    plan = chunk_plan()
    cidx = {n: i for i, (n, _) in enumerate(plan)}
    NCH = len(plan)

    def din(name, shape, dt=F32):
        return nc.dram_tensor(name, list(shape), dt, kind="ExternalInput").ap()

    xT = din("xT", [D, S])
    pos_in = din("pos", [1, S], I32)
    metaT = din("metaT", [D, NMETA])
    norm_gain = din("norm_gain", [n_layers, D])
    w_in = din("w_in", [n_layers, D, D_IN])
    pool_w_group = din("pool_w_group", [n_layers, 4, 128, 128])
    pool_scale = din("pool_scale", [n_layers, 512])
    pool_w_up = din("pool_w_up", [n_layers, 512, D])
    q_a_norm_gain = din("q_a_norm_gain", [n_layers, 768])
    kv_a_norm_gain = din("kv_a_norm_gain", [n_layers, 256])
    w_q_b = din("w_q_b", [n_layers, 768, 768])
    w_kv_b = din("w_kv_b", [n_layers, 256, 1024])
    q_norm_gain = din("q_norm_gain", [n_layers, 96])
    k_norm_gain = din("k_norm_gain", [n_layers, 96])
    mla_w_up = din("mla_w_up", [n_layers, 512, D])
    w_out = din("w_out", [n_layers, D, D])
    cst = din("cst", [128, 160])
    yT = nc.dram_tensor("yT", [D, S], F32, kind="ExternalOutput").ap()
    metaT_out = nc.dram_tensor("metaT_out", [D, NMETA], F32, kind="ExternalOutput").ap() if meta_out else None

    def dscr(name, shape, dt):
        return nc.dram_tensor(name, list(shape), dt, kind="Internal").ap()

    wsc = dscr("wsc", [n_layers, NCH, 128, WSLOT], BF16)
    xs_meta = dscr("xs_meta", [128, 8, NMETA], F32)
    xs = dscr("xs", [n_tiles, 128, 8, TW], F32)
    kT_d = dscr("kT_d", [NKB, 96, NH, 128], BF16)
    v_d = dscr("v_d", [NKB, 128, NH, 128], BF16)
    cos_d = dscr("cos_d", [32, L], F32)
    sin_d = dscr("sin_d", [32, L], F32)

    with ExitStack() as stack:
        k = K(nc, stack)
        pe, act, dve, pool, sp = k.pe, k.act, k.dve, k.pool, k.sp
        sb = k.sb

        cst_t = sb("cst_t", [128, 160], F32)
        ones_bf = sb("ones_bf", [128, 128], BF16)
        tri_bf = sb("tri_bf", [128, 128], BF16)
        gn_t = sb("gn_t", [128, n_layers, 8], F32)
        gqa_t = sb("gqa_t", [128, n_layers, 6], F32)
        gkva_t = sb("gkva_t", [128, n_layers, 2], F32)
        psc_t = sb("psc_t", [128, n_layers, 4], F32)
        gq_t = sb("gq_t", [96, n_layers, 2], F32)
        gk_t = sb("gk_t", [96, n_layers, 2], F32)
        xt = [sb("xt%d" % i, [128, 8, TW], F32) for i in range(2)]
        hT = sb("hT", [128, 8, TW], BF16)
        sq = [sb("sq%d" % i, [128, TW], BF16) for i in range(2)]
        sqk = [sb("sqk%d" % i, [96, TW], BF16) for i in range(2)]
        rs = [sb("rs%d" % i, [128, TW], F32) for i in range(3)]
        ckv_raw = sb("ckv_raw", [128, 2, TW], F32)
        ckvn = sb("ckvn", [128, 2, TW], BF16)
        kr_raw = sb("kr_raw", [96, 2, TW], F32)
        kR = sb("kR", [96, TW], F32)
        tmp32 = [sb("tmp32_%d" % i, [96, TW], F32) for i in range(2)]
        cs_t2 = [sb("cs_t%d" % i, [96, 2, TW], F32) for i in range(2)]
        cg_t = sb("cg_t", [96, 4, TW], F32)
        kT_t = sb("kT_t", [96, 4, NH, 128], BF16)
        va_t = sb("va_t", [128, 4, NH, 128], BF16)
        cqn = sb("cqn", [128, 6, TW], BF16)
        qT = sb("qT", [96, NH, TW], BF16)
        cq_raw = sb("cq_raw", [128, 6, TW], F32)
        zm = sb("zm", [128, 4, TW], F32)
        zp = cq_raw[:, 0:4, :]
        u_t = sb("u_t", [128, 4, 16 + TW], F32)
        pt = [sb("pt%d" % i, [128, 16 + TW], F32) for i in range(2)]
        pg = sb("pg", [128, 4, TW], BF16)
        om = sb("om", [128, 4, TW], BF16)
        rec = [sb("rec%d" % i, [128, TW], F32) for i in range(2)]
        otmp = [sb("otmp%d" % i, [128, TW], F32) for i in range(1)] * 2
        sg = [sb("sg%d" % i, [128, TW], F32) for i in range(4)]
        mt = [sb("mt%d" % i, [128, TW], F32) for i in range(2)]
        merged = sb("merged", [128, 8, TW], BF16)
        pbuf = [sb("pbuf%d" % i, [128, TW], BF16) for i in range(4)]
        mixed = pbuf
        posf = rs[0][0:32]
        ang = [rs[1][0:32], rs[2][0:32]]
        nq = tmp32[0][0:32]
        posi = tmp32[1][0:32].bitcast(I32)
        NW = 5
        wring = [sb("wr%d" % i, [128, WSLOT], BF16) for i in range(NW)]
        NKV = 4
        kslot = [sb("ks%d" % i, [96, 4, 128], BF16) for i in range(NKV)]
        vslot = [sb("vs%d" % i, [128, 4, 128], BF16) for i in range(NKV)]
        banks = [k.ps("bank%d" % i) for i in range(8)]

        def bl(name, n):
            return [Buf("%s%d" % (name, i)) for i in range(n)]

        b_cst, b_ones, b_tri, b_gains = Buf(), Buf(), Buf(), Buf()
        b_xt = [bl("xt", 8), bl("xt", 8)]
        b_hT = bl("hT", 8)
        b_sq, b_sqk = bl("sq", 2), bl("sqk", 2)
        b_sqk_rope = bl("sqkr", 2)
        b_rs = bl("rs", 3)
        b_ckv_raw, b_ckvn = bl("ckvr", 2), bl("ckvn", 2)
        b_kr_raw, b_kR = Buf(), Buf()
        b_tmp32 = bl("tmp32", 2)
        b_cs2, b_cg = [Buf(), Buf()], Buf()
        b_kT, b_va = Buf(), Buf()
        b_va_ones = Buf()
        b_cq_raw, b_cqn = bl("cqr", 6), bl("cqn", 6)
        b_qT = bl("qT", NH)
        b_zm = bl("zm", 4)
        b_zp = b_cq_raw[0:4]
        b_u, b_uh = bl("u", 4), bl("uh", 4)
        b_pt = bl("pt", 2)
        b_pg, b_om = bl("pg", 4), bl("om", 4)
        b_rec, b_otmp = bl("rec", 2), bl("otmp", 1) * 2
        b_sg, b_mt = bl("sg", 4), bl("mt", 2)
        b_merged = bl("merged", 8)
        b_pbuf = bl("pbuf", 4)
        b_mixed = b_pbuf
        b_wring, b_kslot, b_vslot = bl("wr", NW), bl("ks", NKV), bl("vs", NKV)
        b_banks = bl("bank", 8)
        b_posi, b_posf, b_ang = b_tmp32[1], b_rs[0], [b_rs[1], b_rs[2]]
        b_nq = b_tmp32[0]
        b_wsc = [[Buf() for _ in range(NCH)] for _ in range(n_layers)]
        b_xs_meta = Buf()
        b_xs = bl("xs", n_tiles)
        b_kvd = bl("kvd", n_tiles + 1)
        b_csd = bl("csd", n_tiles + 1)
        wsem = [k.new_sem("w") for _ in range(NW)]
        ksem = [k.new_sem("ks") for _ in range(NKV)]
        vsem = [k.new_sem("vs") for _ in range(NKV)]
        ld_rings = {"sp": Ring([k.new_sem("ld") for _ in range(4)]), "pool": Ring([k.new_sem("lq") for _ in range(2)])}

        def ld(q, out, in_, reads=(), writes=(), **kw):
            s = ld_rings[q.name].next()
            return k.dma(q, out, in_, reads=reads, writes=writes, sem=s, extra=[(s, s.cnt)] if s.cnt else [], **kw)

        ld(sp, cst_t[:], cst[:, :], writes=[b_cst])
        k.op(dve, lambda: nc.vector.memset(ones_bf[:], 1.0), writes=[b_ones])
        k.op(dve, lambda: nc.vector.tensor_copy(out=tri_bf[:], in_=cst_t[:, 0:128]), reads=[b_cst], writes=[b_tri])
        k.op(dve, lambda: nc.vector.memset(va_t[:], 1.0), writes=[b_va, b_va_ones])
        k.op(dve, lambda: nc.vector.memset(kT_t[:], 0.0), writes=[b_kT])
        for i_ in range(2):
            k.op(dve, lambda i_=i_: nc.vector.memset(pt[i_][:], 0.0), writes=[b_pt[i_]])
        for g in range(4):
            k.op(dve, lambda g=g: nc.vector.memset(u_t[:, g, 0:16], 0.0), writes=[b_uh[g]])

        def colload(dst, src2d, nchunk):
            ld(pool, dst, src2d.rearrange("l (c p o) -> p l c o", p=128, o=1)[:, :, :, 0], writes=[b_gains],
               allow_slow_non_contiguous=True)

        colload(gn_t[:], norm_gain, 8)
        colload(gqa_t[:], q_a_norm_gain, 6)
        colload(gkva_t[:], kv_a_norm_gain, 2)
        colload(psc_t[:], pool_scale, 4)
        for (gt, src) in ((gq_t, q_norm_gain), (gk_t, k_norm_gain)):
            srcT = src.rearrange("l (d o) -> d l o", o=1)[:, :, 0]
            ld(pool, gt[0:96, :, 0], srcT[0:96, :], writes=[b_gains], allow_slow_non_contiguous=True)
            ld(pool, gt[64:80, :, 1], srcT[80:96, :], writes=[b_gains], allow_slow_non_contiguous=True)
            ld(pool, gt[80:96, :, 1], srcT[64:80, :], writes=[b_gains], allow_slow_non_contiguous=True)

        def cast(l, name, col0, out_view, in_view):
            ci = cidx[name]
            k.dma(pool, out_view(wsc[l, ci]), in_view, writes=[b_wsc[l][ci]])

        def emit_casts(l):
            def std(name, w2d, c0, kc, m, off=0, mtot=None):
                mtot_ = mtot or m
                ci = cidx[name]
                o = wsc[l, ci][:, 0:kc * mtot_].rearrange("p (c m) -> p c m", m=mtot_)[:, :, off:off + m]
                i = w2d.rearrange("(c p) n -> p c n", p=128)[:, :, c0:c0 + m]
                k.dma(pool, o, i, writes=[b_wsc[l][ci]])

            wi = w_in[l]
            std("ckv0", wi, C_CKV, 8, 128)
            std("ckv1", wi, C_CKV + 128, 8, 128)
            std("kr", wi, C_CKV, 8, 64, off=0, mtot=192)
            std("kr", wi, C_KR, 8, 32, off=64, mtot=192)
            std("kr", wi, C_CKV, 8, 64, off=96, mtot=192)
            std("kr", wi, C_KR + 16, 8, 16, off=160, mtot=192)
            std("kr", wi, C_KR, 8, 16, off=176, mtot=192)
            wkv = w_kv_b[l]
            for h in range(NH):
                ci = cidx["kvbk"]
                ov = wsc[l, ci][:, 0:2 * NH * 64].rearrange("p (c h m) -> p c h m", h=NH, m=64)
                iv = wkv.rearrange("(c p) n -> p c n", p=128)
                k.dma(pool, ov[:, :, h, :], iv[:, :, h * 128:h * 128 + 64], writes=[b_wsc[l][ci]])
                ci = cidx["kvbv"]
                ov = wsc[l, ci][:, 0:2 * 512].rearrange("p (c h m) -> p c h m", h=NH, m=64)
                k.dma(pool, ov[:, :, h, :], iv[:, :, h * 128 + 64:h * 128 + 128], writes=[b_wsc[l][ci]])
            for j in range(6):
                std("cq%d" % j, wi, C_CQ + 128 * j, 8, 128)
            wq = w_q_b[l]
            for h in range(NH):
                n = "qb%d" % h
                std(n, wq, h * 96, 6, 96, off=0, mtot=192)
                std(n, wq, h * 96, 6, 64, off=96, mtot=192)
                std(n, wq, h * 96 + 80, 6, 16, off=160, mtot=192)
                std(n, wq, h * 96 + 64, 6, 16, off=176, mtot=192)
            for c in range(4):
                std("zm%d" % c, wi, C_ZM + 128 * c, 8, 128)
            for g in range(4):
                std("u%d" % g, wi, C_U + 128 * g, 8, 128)
                std("zp%d" % g, wi, C_ZP + 128 * g, 8, 128)
            ci = cidx["pg"]
            k.dma(pool, wsc[l, ci][:, 0:512].rearrange("p (g m) -> p g m", m=128),
                  pool_w_group[l].rearrange("g p m -> p g m"), writes=[b_wsc[l][ci]])
            for m in range(8):
                std("gp%d" % m, wi, C_GP + 128 * m, 8, 128)
                std("gm%d" % m, wi, C_GM + 128 * m, 8, 128)
                ci = cidx["up%d" % m]
                ov = wsc[l, ci][:, 0:1024].rearrange("p (t c m) -> p t c m", t=2, m=128)
                k.dma(pool, ov[:, 0], pool_w_up[l].rearrange("(c p) n -> p c n", p=128)[:, :, 128 * m:128 * m + 128],
                      writes=[b_wsc[l][ci]])
                k.dma(pool, ov[:, 1], mla_w_up[l].rearrange("(c p) n -> p c n", p=128)[:, :, 128 * m:128 * m + 128],
                      writes=[b_wsc[l][ci]])
            for m in range(8):
                std("wo%d" % m, w_out[l], 128 * m, 8, 128)

        wstate = {"i": 0}

        def wload(l, name):
            ci = cidx[name]
            n = plan[ci][1]
            s = wstate["i"] % NW
            wstate["i"] += 1
            k.dma(sp, wring[s][:, 0:n], wsc[l, ci][:, 0:n], reads=[b_wsc[l][ci]], writes=[b_wring[s]], sem=wsem[s])
            return wring[s], b_wring[s]

        def rope_tables():
            invf = cst_t[0:32, 128:129]
            sgn = cst_t[0:32, 129:130]
            for t in range(n_tiles + 1):
                N = NMETA if t == 0 else TW
                t0 = 0 if t == 0 else NMETA + (t - 1) * TW
                if t == 0:
                    k.op(dve, lambda: nc.vector.tensor_copy(out=posf[:, 0:N], in_=cst_t[0:32, 132:148]),
                         reads=[b_cst], writes=[b_posf])
                else:
                    ld(sp, posi[:, 0:N], pos_in[0, (t - 1) * TW:t * TW].partition_broadcast(32), writes=[b_posi])
                    k.op(dve, lambda: nc.vector.tensor_copy(out=posf[:, 0:N], in_=posi[:, 0:N]),
                         reads=[b_posi], writes=[b_posf])
                    k.op(dve, lambda: nc.vector.tensor_scalar_add(out=posf[:, 0:N], in0=posf[:, 0:N], scalar1=float(NMETA)),
                         reads=[b_posf], writes=[b_posf])
                k.op(dve, lambda: nc.vector.tensor_scalar_mul(out=posf[:, 0:N], in0=posf[:, 0:N], scalar1=invf),
                     reads=[b_posf, b_cst], writes=[b_posf])
                C1 = 6.28125
                C2 = 2.0 * math.pi - C1
                for which, shift in ((0, math.pi * 0.5), (1, 0.0)):
                    a = ang[which]
                    ba = b_ang[which]
                    k.op(dve, lambda a=a, shift=shift: nc.vector.tensor_scalar_add(out=a[:, 0:N], in0=posf[:, 0:N], scalar1=shift),
                         reads=[b_posf], writes=[ba])
                    k.op(dve, lambda a=a: nc.vector.tensor_scalar_mul(out=nq[:, 0:N], in0=a[:, 0:N], scalar1=1.0 / (2.0 * math.pi)),
                         reads=[ba], writes=[b_nq])
                    k.op(dve, lambda: nc.vector.tensor_copy(out=posi[:, 0:N], in_=nq[:, 0:N]), reads=[b_nq], writes=[b_posi])
                    k.op(dve, lambda: nc.vector.tensor_copy(out=nq[:, 0:N], in_=posi[:, 0:N]), reads=[b_posi], writes=[b_nq])
                    k.op(dve, lambda a=a: nc.vector.scalar_tensor_tensor(out=a[:, 0:N], in0=nq[:, 0:N], scalar=-C1, in1=a[:, 0:N], op0=ALU.mult, op1=ALU.add),
                         reads=[b_nq, ba], writes=[ba])
                    k.op(dve, lambda a=a: nc.vector.scalar_tensor_tensor(out=a[:, 0:N], in0=nq[:, 0:N], scalar=-C2, in1=a[:, 0:N], op0=ALU.mult, op1=ALU.add),
                         reads=[b_nq, ba], writes=[ba])
                    k.op(dve, lambda a=a: nc.vector.tensor_single_scalar(out=nq[:, 0:N], in_=a[:, 0:N], scalar=math.pi, op=ALU.is_gt),
                         reads=[ba], writes=[b_nq])
                    k.op(dve, lambda a=a: nc.vector.scalar_tensor_tensor(out=a[:, 0:N], in0=nq[:, 0:N], scalar=-2.0 * math.pi, in1=a[:, 0:N], op0=ALU.mult, op1=ALU.add),
                         reads=[b_nq, ba], writes=[ba])
                    k.op(dve, lambda a=a: nc.vector.tensor_scalar_max(out=a[:, 0:N], in0=a[:, 0:N], scalar1=-3.1415925), reads=[ba], writes=[ba])
                    k.op(dve, lambda a=a: nc.vector.tensor_scalar_min(out=a[:, 0:N], in0=a[:, 0:N], scalar1=3.1415925), reads=[ba], writes=[ba])
                    if which == 0:
                        k.op(act, lambda a=a: nc.scalar.activation(out=a[:, 0:N], in_=a[:, 0:N], func=AF.Sin),
                             reads=[ba], writes=[ba])
                        k.dma(pool, cos_d[:, t0:t0 + N], a[:, 0:N], reads=[ba], writes=[b_csd[t]])
                    else:
                        k.op(act, lambda a=a: nc.scalar.activation(out=a[:, 0:N], in_=a[:, 0:N], func=AF.Sin, scale=sgn),
                             reads=[ba, b_cst], writes=[ba])
                        k.dma(pool, sin_d[:, t0:t0 + N], a[:, 0:N], reads=[ba], writes=[b_csd[t]])

        gring = Ring([0, 1, 2, 3, 4, 5, 6])

        def gbank():
            i = gring.next()
            return banks[i], b_banks[i]

        def proj(l, name, M, N, rhs_list, rhs_bufs, coloff=0, mtot=None, kc=None):
            w, bw = wload(l, name)
            kc = kc or len(rhs_list)
            mt_ = mtot or M
            wv = w[:, 0:kc * mt_].rearrange("p (c m) -> p c m", m=mt_)
            bank, bb = gbank()
            for c in range(kc):
                k.op(pe, lambda c=c: nc.tensor.matmul(bank[0:M, 0:N], lhsT=wv[:, c, coloff:coloff + M], rhs=rhs_list[c],
                                                       start=(c == 0), stop=(c == kc - 1)),
                     reads=[bw, rhs_bufs[c]], writes=[bb])
            return bank, bb, wv, bw

        def rstd_from(bank, bb, P, N, inv_n, dst, bdst):
            k.op(act, lambda: nc.scalar.activation(out=dst[0:P, 0:N], in_=bank[0:P, 0:N], func=AF.Ln, bias=cst_t[0:P, 148:149], scale=inv_n),
                 reads=[bb, b_cst], writes=[bdst])
            k.op(act, lambda: nc.scalar.activation(out=dst[0:P, 0:N], in_=dst[0:P, 0:N], func=AF.Exp, scale=-0.5), reads=[bdst], writes=[bdst])

        sq_ring = Ring([0, 1])
        sqk_ring = Ring([0, 1])
        rs_ring = Ring([0, 1, 2])
        raw_ring = Ring([0, 1])

        def emit_loads(l, t, xb):
            N = NMETA if t == 0 else TW
            x_t, bx = xt[xb], b_xt[xb]
            if l == 0:
                src = metaT.rearrange("(c p) n -> p c n", p=128) if t == 0 else \
                    xT.rearrange("(c p) n -> p c n", p=128)[:, :, (t - 1) * TW:t * TW]
                ld(sp, x_t[:, :, 0:N], src, writes=bx)
            else:
                if t == 0:
                    ld(sp, x_t[:, :, 0:N], xs_meta[:, :, :], reads=[b_xs_meta], writes=bx)
                else:
                    ld(sp, x_t[:, :, 0:N], xs[t - 1], reads=[b_xs[t - 1]], writes=bx)
            t0 = 0 if t == 0 else NMETA + (t - 1) * TW
            ld(sp, cs_t2[xb][64:96, 0, 0:N], cos_d[:, t0:t0 + N], reads=[b_csd[t]], writes=[b_cs2[xb]])
            ld(sp, cs_t2[xb][64:96, 1, 0:N], sin_d[:, t0:t0 + N], reads=[b_csd[t]], writes=[b_cs2[xb]])

        def tile_layer(l, t, xb):
            N = NMETA if t == 0 else TW
            NB = 1 if t == 0 else 4
            last = (l == n_layers - 1)
            x_t, bx = xt[xb], b_xt[xb]
            cs_t, b_cs = cs_t2[xb], b_cs2[xb]
            t0 = 0 if t == 0 else NMETA + (t - 1) * TW
            for j, (gt, col, tab) in enumerate(((gq_t, 0, 0), (gq_t, 1, 1), (gk_t, 0, 0), (gk_t, 1, 1))):
                k.op(dve, lambda j=j, gt=gt, col=col, tab=tab: nc.vector.tensor_scalar_mul(
                    out=cg_t[64:96, j, 0:N], in0=cs_t[64:96, tab, 0:N], scalar1=gt[64:96, l, col:col + 1]),
                    reads=[b_cs, b_gains], writes=[b_cg])

            bank, bb = gbank()
            for c in range(8):
                i = sq_ring.next()
                k.op(act, lambda c=c, i=i: nc.scalar.activation(out=sq[i][:, 0:N], in_=x_t[:, c, 0:N], func=AF.Square),
                     reads=[bx[c]], writes=[b_sq[i]])
                k.op(pe, lambda c=c, i=i: nc.tensor.matmul(bank[:, 0:N], lhsT=ones_bf[:, :], rhs=sq[i][:, 0:N], start=(c == 0), stop=(c == 7)),
                     reads=[b_ones, b_sq[i]], writes=[bb])
            r = rs_ring.next()
            rstd_from(bank, bb, 128, N, 1.0 / D, rs[r], b_rs[r])
            for c in range(8):
                k.op(dve, lambda c=c: nc.vector.scalar_tensor_tensor(out=hT[:, c, 0:N], in0=x_t[:, c, 0:N], scalar=gn_t[:, l, c:c + 1],
                                                                     in1=rs[r][:, 0:N], op0=ALU.mult, op1=ALU.mult),
                     reads=[bx[c], b_rs[r], b_gains], writes=[b_hT[c]])
            h_rhs = [hT[:, c, 0:N] for c in range(8)]

            if stage < 3.1005:
                return None
            ssb, bssb = banks[7], b_banks[7]
            if stage < 3.1015:
                wload(l, "ckv0")
                return None
            if stage < 3.1025:
                proj(l, "ckv0", 128, N, h_rhs, b_hT)
                return None
            sspend = None
            for j in range(2):
                bank, bb, _, _ = proj(l, "ckv%d" % j, 128, N, h_rhs, b_hT)
                k.op(dve, lambda j=j, bank=bank: nc.vector.tensor_copy(out=ckv_raw[:, j, 0:N], in_=bank[:, 0:N]),
                     reads=[bb], writes=[b_ckv_raw[j]])
                if stage < 3.1035:
                    if j == 1:
                        return None
                    continue
                i = sq_ring.next()
                k.op(act, lambda i=i, j=j: nc.scalar.activation(out=sq[i][:, 0:N], in_=ckv_raw[:, j, 0:N], func=AF.Square),
                     reads=[b_ckv_raw[j]], writes=[b_sq[i]])

                def sstail(j=j, i=i):
                    k.op(pe, lambda: nc.tensor.matmul(ssb[:, 0:N], lhsT=ones_bf[:, :], rhs=sq[i][:, 0:N], start=(j == 0), stop=(j == 1)),
                         reads=[b_ones, b_sq[i]], writes=[bssb])
                if sspend is not None:
                    sspend()
                sspend = sstail
            if sspend is not None:
                sspend()
            if stage < 3.1045:
                return None
            r = rs_ring.next()
            rstd_from(ssb, bssb, 128, N, 1.0 / 256, rs[r], b_rs[r])
            for j in range(2):
                k.op(dve, lambda j=j: nc.vector.scalar_tensor_tensor(out=ckvn[:, j, 0:N], in0=ckv_raw[:, j, 0:N], scalar=gkva_t[:, l, j:j + 1],
                                                                     in1=rs[r][:, 0:N], op0=ALU.mult, op1=ALU.mult),
                     reads=[b_ckv_raw[j], b_rs[r], b_gains], writes=[b_ckvn[j]])
            if stage < 3.1105:
                return None
            w, bw = wload(l, "kr")
            wv = w[:, 0:1536].rearrange("p (c m) -> p c m", m=192)
            for v_ in range(2):
                bank, bb = gbank()
                for c in range(8):
                    k.op(pe, lambda c=c, v_=v_, bank=bank: nc.tensor.matmul(bank[0:96, 0:N], lhsT=wv[:, c, 96 * v_:96 * v_ + 96], rhs=h_rhs[c],
                                                                         start=(c == 0), stop=(c == 7)),
                         reads=[bw, b_hT[c]], writes=[bb])
                k.op(dve, lambda v_=v_, bank=bank: nc.vector.tensor_copy(out=kr_raw[64:96, v_, 0:N], in_=bank[64:96, 0:N]),
                     reads=[bb], writes=[b_kr_raw])
                if v_ == 0:
                    for i in range(2):
                        k.op(act, lambda i=i: nc.scalar.activation(out=sqk[i][64:96, 0:N], in_=kr_raw[64:96, 0, 0:N], func=AF.Square),
                             reads=[b_kr_raw], writes=[b_sqk_rope[i]])
            k.op(dve, lambda: nc.vector.tensor_tensor(out=tmp32[0][64:96, 0:N], in0=kr_raw[64:96, 0, 0:N], in1=cg_t[64:96, 2, 0:N], op=ALU.mult),
                 reads=[b_kr_raw, b_cg], writes=[b_tmp32[0]])
            k.op(dve, lambda: nc.vector.tensor_tensor(out=kR[64:96, 0:N], in0=kr_raw[64:96, 1, 0:N], in1=cg_t[64:96, 3, 0:N], op=ALU.mult),
                 reads=[b_kr_raw, b_cg], writes=[b_kR])
            k.op(dve, lambda: nc.vector.tensor_tensor(out=kR[64:96, 0:N], in0=kR[64:96, 0:N], in1=tmp32[0][64:96, 0:N], op=ALU.add),
                 reads=[b_kR, b_tmp32[0]], writes=[b_kR])
            if stage < 3.1205:
                return None
            w, bw = wload(l, "kvbk")
            wkk = w[:, 0:2 * NH * 64].rearrange("p (c h m) -> p c h m", h=NH, m=64)
            ckv_rhs = [ckvn[:, c, 0:N] for c in range(2)]

            def ktv(rows, h):
                if t == 0:
                    return kT_t[rows, 0, h, 0:N]
                return kT_t[rows, :, h, :]

            def as_blk(ap):
                return ap if t == 0 else ap.rearrange("p (b c) -> p b c", c=128)

            kpend = None
            for h in range(NH):
                bank, bb = gbank()
                for c in range(2):
                    k.op(pe, lambda c=c, h=h, bank=bank: nc.tensor.matmul(bank[0:64, 0:N], lhsT=wkk[:, c, h, :], rhs=ckv_rhs[c], start=(c == 0), stop=(c == 1)),
                         reads=[bw, b_ckvn[c]], writes=[bb])
                ri_ = raw_ring.next()
                k.op(dve, lambda ri_=ri_, bank=bank: nc.vector.tensor_copy(out=rec[ri_][0:64, 0:N], in_=bank[0:64, 0:N]), reads=[bb], writes=[b_rec[ri_]])
                i = sqk_ring.next()
                k.op(act, lambda i=i, ri_=ri_: nc.scalar.activation(out=sqk[i][0:64, 0:N], in_=rec[ri_][0:64, 0:N], func=AF.Square),
                     reads=[b_rec[ri_]], writes=[b_sqk[i]])

                def ktail(h=h, ri_=ri_, i=i):
                    bank2, bb2 = gbank()
                    k.op(pe, lambda: nc.tensor.matmul(bank2[0:96, 0:N], lhsT=ones_bf[0:96, 0:96], rhs=sqk[i][0:96, 0:N], start=True, stop=True),
                         reads=[b_ones, b_sqk[i], b_sqk_rope[i]], writes=[bb2])
                    r = rs_ring.next()
                    rstd_from(bank2, bb2, 96, N, 1.0 / 96, rs[r], b_rs[r])
                    k.op(dve, lambda: nc.vector.scalar_tensor_tensor(
                        out=ktv(slice(0, 64), h), in0=as_blk(rec[ri_][0:64, 0:N]), scalar=gk_t[0:64, l, 0:1],
                        in1=as_blk(rs[r][0:64, 0:N]), op0=ALU.mult, op1=ALU.mult),
                        reads=[b_rec[ri_], b_rs[r], b_gains], writes=[b_kT])
                    k.op(pool, lambda: nc.gpsimd.tensor_tensor(out=ktv(slice(64, 96), h), in0=as_blk(kR[64:96, 0:N]), in1=as_blk(rs[r][64:96, 0:N]), op=ALU.mult),
                         reads=[b_kR, b_rs[r]], writes=[b_kT])
                if kpend is not None:
                    kpend()
                kpend = ktail
            kpend()
            if stage < 3.1305:
                return None
            w, bw = wload(l, "kvbv")
            wvv = w[:, 0:1024].rearrange("p (c n) -> p c n", n=512)
            for tb in range(NB):
                nt = N if t == 0 else 128
                bank, bb = gbank()
                for c in range(2):
                    k.op(pe, lambda c=c, tb=tb, bank=bank: nc.tensor.matmul(bank[0:nt, 0:512], lhsT=ckvn[:, c, tb * 128:tb * 128 + nt], rhs=wvv[:, c, :],
                                                                         start=(c == 0), stop=(c == 1)),
                         reads=[bw, b_ckvn[c]], writes=[bb])
                pv = bank[0:nt, 0:512].rearrange("p (h two d) -> p h two d", two=2, d=64)
                vv = va_t[0:nt, tb].rearrange("p (h two) c -> p h two c", two=2)
                k.op(act, lambda pv=pv, vv=vv: nc.scalar.copy(out=vv[:, :, 0, 0:64], in_=pv[:, :, 0, :]), reads=[bb, b_va_ones], writes=[b_va])
                k.op(dve, lambda pv=pv, vv=vv: nc.vector.tensor_copy(out=vv[:, :, 1, 64:128], in_=pv[:, :, 1, :]), reads=[bb, b_va_ones], writes=[b_va])
            if stage < 3.1405:
                return None
            n_st = 4 if stage >= 3.2 else int(round((stage - 3.14) * 1000))
            if t == 0:
                if n_st >= 1:
                    k.dma(pool, kT_d[0].rearrange("p h c -> p (h c)"), kT_t[:, 0].rearrange("p h c -> p (h c)"), reads=[b_kT], writes=[b_kvd[0]])
                if n_st >= 2:
                    k.dma(pool, v_d[0][0:N].rearrange("p h c -> p (h c)"), va_t[0:N, 0].rearrange("p h c -> p (h c)"), reads=[b_va], writes=[b_kvd[0]])
            else:
                kb0 = 1 + 4 * (t - 1)
                if n_st >= 3:
                    k.dma(pool, kT_d[kb0:kb0 + 4].rearrange("b p h c -> p b (h c)"), kT_t[:].rearrange("p b h c -> p b (h c)"),
                          reads=[b_kT], writes=[b_kvd[t]])
                if n_st >= 4:
                    k.dma(pool, v_d[kb0:kb0 + 4].rearrange("b p h c -> p b (h c)"), va_t[:].rearrange("p b h c -> p b (h c)"),
                          reads=[b_va], writes=[b_kvd[t]])

            if stage < 3.25:
                return None
            qspend = None
            for j in range(6):
                bank, bb, _, _ = proj(l, "cq%d" % j, 128, N, h_rhs, b_hT)
                k.op(dve, lambda j=j, bank=bank: nc.vector.tensor_copy(out=cq_raw[:, j, 0:N], in_=bank[:, 0:N]), reads=[bb], writes=[b_cq_raw[j]])
                i = sq_ring.next()
                k.op(act, lambda i=i, j=j: nc.scalar.activation(out=sq[i][:, 0:N], in_=cq_raw[:, j, 0:N], func=AF.Square), reads=[b_cq_raw[j]], writes=[b_sq[i]])

                def qstail(j=j, i=i):
                    k.op(pe, lambda: nc.tensor.matmul(ssb[:, 0:N], lhsT=ones_bf[:, :], rhs=sq[i][:, 0:N], start=(j == 0), stop=(j == 5)),
                         reads=[b_ones, b_sq[i]], writes=[bssb])
                if qspend is not None:
                    qspend()
                qspend = qstail
            qspend()
            r = rs_ring.next()
            rstd_from(ssb, bssb, 128, N, 1.0 / 768, rs[r], b_rs[r])
            for j in range(6):
                k.op(dve, lambda j=j: nc.vector.scalar_tensor_tensor(out=cqn[:, j, 0:N], in0=cq_raw[:, j, 0:N], scalar=gqa_t[:, l, j:j + 1],
                                                                     in1=rs[r][:, 0:N], op0=ALU.mult, op1=ALU.mult),
                     reads=[b_cq_raw[j], b_rs[r], b_gains], writes=[b_cqn[j]])
            cq_rhs = [cqn[:, c, 0:N] for c in range(6)]
            qpend = None
            for h in range(NH):
                w, bw = wload(l, "qb%d" % h)
                wq_ = w[:, 0:1152].rearrange("p (c m) -> p c m", m=192)
                bq, bbq = gbank()
                for c in range(6):
                    k.op(pe, lambda c=c, bq=bq: nc.tensor.matmul(bq[0:96, 0:N], lhsT=wq_[:, c, 0:96], rhs=cq_rhs[c], start=(c == 0), stop=(c == 5)),
                         reads=[bw, b_cqn[c]], writes=[bbq])
                bs_, bbs = gbank()
                for c in range(6):
                    k.op(pe, lambda c=c, bs_=bs_: nc.tensor.matmul(bs_[0:96, 0:N], lhsT=wq_[:, c, 96:192], rhs=cq_rhs[c], start=(c == 0), stop=(c == 5)),
                         reads=[bw, b_cqn[c]], writes=[bbs])
                ri_ = raw_ring.next()
                k.op(dve, lambda ri_=ri_, bq=bq: nc.vector.tensor_copy(out=rec[ri_][0:96, 0:N], in_=bq[0:96, 0:N]), reads=[bbq], writes=[b_rec[ri_]])
                i = sqk_ring.next()
                k.op(act, lambda i=i, ri_=ri_: nc.scalar.activation(out=sqk[i][0:96, 0:N], in_=rec[ri_][0:96, 0:N], func=AF.Square),
                     reads=[b_rec[ri_]], writes=[b_sqk[i], b_sqk_rope[i]])
                def qtail(h=h, i=i, ri_=ri_, bq=bq, bbq=bbq, bs_=bs_, bbs=bbs):
                    b3, bb3 = gbank()
                    k.op(pe, lambda i=i, b3=b3: nc.tensor.matmul(b3[0:96, 0:N], lhsT=ones_bf[0:96, 0:96], rhs=sqk[i][0:96, 0:N], start=True, stop=True),
                         reads=[b_ones, b_sqk[i], b_sqk_rope[i]], writes=[bb3])
                    r = rs_ring.next()
                    rstd_from(b3, bb3, 96, N, 1.0 / 96, rs[r], b_rs[r])
                    k.op(dve, lambda h=h, ri_=ri_, r=r: nc.vector.scalar_tensor_tensor(out=qT[0:64, h, 0:N], in0=rec[ri_][0:64, 0:N], scalar=gq_t[0:64, l, 0:1],
                                                                                    in1=rs[r][0:64, 0:N], op0=ALU.mult, op1=ALU.mult),
                         reads=[b_rec[ri_], b_rs[r], b_gains], writes=[b_qT[h]])
                    k.op(pool, lambda ri_=ri_: nc.gpsimd.tensor_tensor(out=tmp32[0][64:96, 0:N], in0=rec[ri_][64:96, 0:N], in1=cg_t[64:96, 0, 0:N], op=ALU.mult),
                         reads=[b_rec[ri_], b_cg], writes=[b_tmp32[0]])
                    k.op(dve, lambda bs_=bs_: nc.vector.tensor_tensor(out=tmp32[1][64:96, 0:N], in0=bs_[64:96, 0:N], in1=cg_t[64:96, 1, 0:N], op=ALU.mult),
                         reads=[bbs, b_cg], writes=[b_tmp32[1]])
                    k.op(pool, lambda: nc.gpsimd.tensor_tensor(out=tmp32[0][64:96, 0:N], in0=tmp32[0][64:96, 0:N], in1=tmp32[1][64:96, 0:N], op=ALU.add),
                         reads=[b_tmp32[0], b_tmp32[1]], writes=[b_tmp32[0]])
                    k.op(pool, lambda h=h, r=r: nc.gpsimd.tensor_tensor(out=qT[64:96, h, 0:N], in0=tmp32[0][64:96, 0:N], in1=rs[r][64:96, 0:N], op=ALU.mult),
                         reads=[b_tmp32[0], b_rs[r]], writes=[b_qT[h]])
                if qpend is not None:
                    qpend()
                qpend = qtail
            qpend()
            for c in range(4):
                bank, bb, _, _ = proj(l, "zm%d" % c, 128, N, h_rhs, b_hT)
                k.op(act, lambda c=c, bank=bank: nc.scalar.activation(out=zm[:, c, 0:N], in_=bank[:, 0:N], func=AF.Silu), reads=[bb], writes=[b_zm[c]])

            if stage < 3.35:
                return None
            if pending_loads[0] is not None:
                pending_loads[0]()
                pending_loads[0] = None
            scale = 1.0 / math.sqrt(96.0)
            kbs = [(0, NMETA, 0, t == 0)]
            if t > 0:
                kbs += [(1 + j, 128, 0, False) for j in range(4 * (t - 1))]
                kbs += [(1 + 4 * (t - 1) + r_, 128, 128 * r_, True) for r_ in range(4)]
            sring = Ring([0, 1, 2])
            pring = Ring([0, 1, 2, 3])
            for hp in range(2):
                pv_pend = []
                for bi, (kb, nk, c0, diag) in enumerate(kbs):
                    s = wstate.setdefault("kv", 0) % NKV
                    wstate["kv"] += 1
                    tk = 0 if kb == 0 else 1 + (kb - 1) // 4
                    k.dma(sp, kslot[s][:, :, :], kT_d[kb][:, 4 * hp:4 * hp + 4, :], reads=[b_kvd[tk]], writes=[b_kslot[s]], sem=ksem[s])
                    k.dma(sp, vslot[s][0:nk], v_d[kb][0:nk, 4 * hp:4 * hp + 4, :], reads=[b_kvd[tk]], writes=[b_vslot[s]], sem=vsem[s])
                    for hh in range(4):
                        h = 4 * hp + hh
                        si = sring.next()
                        sbk, bsb = banks[si], b_banks[si]
                        k.op(pe, lambda s=s, hh=hh, h=h, sbk=sbk: nc.tensor.matmul(sbk[0:nk, c0:N], lhsT=kslot[s][:, hh, 0:nk], rhs=qT[:, h, c0:N], start=True, stop=True),
                             reads=[b_kslot[s], b_qT[h]], writes=[bsb])
                        pi = pring.next()
                        k.op(act, lambda pi=pi, sbk=sbk: nc.scalar.activation(out=pbuf[pi][0:nk, c0:N], in_=sbk[0:nk, c0:N], func=AF.Exp, scale=scale),
                             reads=[bsb], writes=[b_pbuf[pi]])
                        if diag:
                            wd = min(128, N - c0)
                            k.op(pool, lambda pi=pi, wd=wd: nc.gpsimd.tensor_tensor(out=pbuf[pi][0:nk, c0:c0 + wd], in0=pbuf[pi][0:nk, c0:c0 + wd],
                                                                                    in1=tri_bf[0:nk, 0:wd], op=ALU.mult),
                                 reads=[b_pbuf[pi], b_tri], writes=[b_pbuf[pi]])
                        ob, bob = banks[3 + hh], b_banks[3 + hh]

                        def pv(s=s, hh=hh, pi=pi, ob=ob, bob=bob, nk=nk, c0=c0, bi=bi):
                            k.op(pe, lambda: nc.tensor.matmul(ob[:, c0:N], lhsT=vslot[s][0:nk, hh, :], rhs=pbuf[pi][0:nk, c0:N],
                                                              start=(bi == 0), stop=(bi == len(kbs) - 1)),
                                 reads=[b_vslot[s], b_pbuf[pi]], writes=[bob])
                        pv_pend.append(pv)
                        if len(pv_pend) > 2:
                            pv_pend.pop(0)()
                while pv_pend:
                    pv_pend.pop(0)()
                for hh in range(4):
                    h = 4 * hp + hh
                    ob, bob = banks[3 + hh], b_banks[3 + hh]
                    pr, par = h // 2, h % 2
                    orow = slice(0, 64) if par == 0 else slice(64, 128)
                    drow = slice(64, 128) if par == 0 else slice(0, 64)
                    ri = hh % 2
                    k.op(act, lambda ob=ob, ri=ri: nc.scalar.activation(out=rec[ri][drow, 0:N], in_=ob[drow, 0:N], func=AF.Ln), reads=[bob], writes=[b_rec[ri]])
                    k.op(act, lambda ri=ri: nc.scalar.activation(out=rec[ri][orow, 0:N], in_=rec[ri][drow, 0:N], func=AF.Exp, scale=-1.0),
                         reads=[b_rec[ri]], writes=[b_rec[ri]])
                    k.op(dve, lambda ob=ob, ri=ri: nc.vector.tensor_tensor(out=otmp[ri][orow, 0:N], in0=ob[orow, 0:N], in1=rec[ri][orow, 0:N], op=ALU.mult),
                         reads=[bob, b_rec[ri]], writes=[b_otmp[ri]])
                    k.op(pool, lambda ri=ri, pr=pr: nc.gpsimd.tensor_tensor(out=om[orow, pr, 0:N], in0=otmp[ri][orow, 0:N], in1=zm[orow, pr, 0:N], op=ALU.mult),
                         reads=[b_otmp[ri], b_zm[pr]], writes=[b_om[pr]])

            if stage < 3.45:
                return None
            for g in range(4):
                bank, bb, _, _ = proj(l, "u%d" % g, 128, N, h_rhs, b_hT)
                k.op(dve, lambda g=g, bank=bank: nc.vector.tensor_copy(out=u_t[:, g, 16:16 + N], in_=bank[:, 0:N]), reads=[bb], writes=[b_u[g]])
            for g in range(4):
                bank, bb, _, _ = proj(l, "zp%d" % g, 128, N, h_rhs, b_hT)
                k.op(act, lambda g=g, bank=bank: nc.scalar.activation(out=zp[:, g, 0:N], in_=bank[:, 0:N], func=AF.Silu), reads=[bb], writes=[b_zp[g]])
            wpgt, bwpg = wload(l, "pg")
            wpgv = wpgt[:, 0:512].rearrange("p (g m) -> p g m", m=128)
            for g in range(4):
                W_ = 16 + N
                src, bsrc = u_t[:, g, 0:W_], [b_u[g], b_uh[g]]
                sh = 1
                pi_ = 0
                while sh < POOL_W[g]:
                    dst, bdst = pt[pi_], b_pt[pi_]
                    k.op(pool, lambda src=src, dst=dst, sh=sh: nc.gpsimd.tensor_tensor(out=dst[:, sh:W_], in0=src[:, sh:W_], in1=src[:, 0:W_ - sh], op=ALU.add),
                         reads=bsrc, writes=[bdst])
                    src, bsrc = dst[:, 0:W_], [bdst]
                    sh *= 2
                    pi_ ^= 1
                if t == 0:
                    k.op(dve, lambda src=src, g=g, pi_=pi_: nc.vector.tensor_tensor(out=pt[pi_][:, 16:16 + N], in0=src[:, 16:16 + N], in1=cst2_t[:, g, :], op=ALU.mult),
                         reads=bsrc + [b_cst], writes=[b_pt[pi_]])
                    k.op(dve, lambda g=g, pi_=pi_: nc.vector.tensor_tensor(out=mixed[g][:, 0:N], in0=pt[pi_][:, 16:16 + N], in1=u_t[:, g, 16:16 + N], op=ALU.subtract),
                         reads=[b_pt[pi_], b_u[g]], writes=[b_mixed[g]])
                else:
                    k.op(dve, lambda src=src, g=g: nc.vector.scalar_tensor_tensor(out=mixed[g][:, 0:N], in0=src[:, 16:16 + N], scalar=1.0 / POOL_W[g],
                                                                                 in1=u_t[:, g, 16:16 + N], op0=ALU.mult, op1=ALU.subtract),
                         reads=bsrc + [b_u[g]], writes=[b_mixed[g]])
                k.op(pool, lambda g=g: nc.gpsimd.tensor_copy(out=u_t[:, g, 0:16], in_=u_t[:, g, N:N + 16]), reads=[b_u[g]], writes=[b_uh[g]])
                bank, bb = gbank()
                k.op(pe, lambda g=g, bank=bank: nc.tensor.matmul(bank[:, 0:N], lhsT=wpgv[:, g, :], rhs=mixed[g][:, 0:N], start=True, stop=True),
                     reads=[bwpg, b_mixed[g]], writes=[bb])
                k.op(dve, lambda g=g, bank=bank: nc.vector.scalar_tensor_tensor(out=pg[:, g, 0:N], in0=bank[:, 0:N], scalar=psc_t[:, l, g:g + 1],
                                                                               in1=zp[:, g, 0:N], op0=ALU.mult, op1=ALU.mult),
                     reads=[bb, b_zp[g], b_gains], writes=[b_pg[g]])

            if stage < 3.55:
                return None
            if last and t == 0 and not meta_out:
                return

            pg_rhs = [pg[:, c, 0:N] for c in range(4)]
            om_rhs = [om[:, c, 0:N] for c in range(4)]
            sgr = Ring([0, 1, 2, 3])
            mtr = Ring([0, 1])
            for m in range(8):
                b1, bb1, _, _ = proj(l, "gp%d" % m, 128, N, h_rhs, b_hT)
                s1 = sgr.next()
                k.op(act, lambda b1=b1, s1=s1: nc.scalar.activation(out=sg[s1][:, 0:N], in_=b1[:, 0:N], func=AF.Sigmoid), reads=[bb1], writes=[b_sg[s1]])
                b2, bb2, _, _ = proj(l, "gm%d" % m, 128, N, h_rhs, b_hT)
                s2 = sgr.next()
                k.op(act, lambda b2=b2, s2=s2: nc.scalar.activation(out=sg[s2][:, 0:N], in_=b2[:, 0:N], func=AF.Sigmoid), reads=[bb2], writes=[b_sg[s2]])
                w, bw = wload(l, "up%d" % m)
                wu = w[:, 0:1024].rearrange("p (t c m) -> p t c m", t=2, m=128)
                b3, bb3 = gbank()
                for c in range(4):
                    k.op(pe, lambda c=c, b3=b3: nc.tensor.matmul(b3[:, 0:N], lhsT=wu[:, 0, c, :], rhs=pg_rhs[c], start=(c == 0), stop=(c == 3)),
                         reads=[bw, b_pg[c]], writes=[bb3])
                m1 = mtr.next()
                k.op(dve, lambda b3=b3, s1=s1, m1=m1: nc.vector.tensor_tensor(out=mt[m1][:, 0:N], in0=b3[:, 0:N], in1=sg[s1][:, 0:N], op=ALU.mult),
                     reads=[bb3, b_sg[s1]], writes=[b_mt[m1]])
                b4, bb4 = gbank()
                for c in range(4):
                    k.op(pe, lambda c=c, b4=b4: nc.tensor.matmul(b4[:, 0:N], lhsT=wu[:, 1, c, :], rhs=om_rhs[c], start=(c == 0), stop=(c == 3)),
                         reads=[bw, b_om[c]], writes=[bb4])
                m2 = mtr.next()
                k.op(dve, lambda b4=b4, s2=s2, m2=m2: nc.vector.tensor_tensor(out=mt[m2][:, 0:N], in0=b4[:, 0:N], in1=sg[s2][:, 0:N], op=ALU.mult),
                     reads=[bb4, b_sg[s2]], writes=[b_mt[m2]])
                k.op(pool, lambda m=m, m1=m1, m2=m2: nc.gpsimd.tensor_tensor(out=merged[:, m, 0:N], in0=mt[m1][:, 0:N], in1=mt[m2][:, 0:N], op=ALU.add),
                     reads=[b_mt[m1], b_mt[m2]], writes=[b_merged[m]])

            if stage < 3.65:
                return None
            mg_rhs = [merged[:, c, 0:N] for c in range(8)]
            for m in range(8):
                bank, bb, _, _ = proj(l, "wo%d" % m, 128, N, mg_rhs, b_merged)
                k.op(dve, lambda m=m, bank=bank: nc.vector.tensor_tensor(out=x_t[:, m, 0:N], in0=x_t[:, m, 0:N], in1=bank[:, 0:N], op=ALU.add),
                     reads=[bb, bx[m]], writes=[bx[m]])
            if last and t == 0:
                return k.dma(pool, metaT_out.rearrange("(c p) n -> p c n", p=128), x_t[:, :, 0:N], reads=bx, writes=[b_xs_meta])
            if last:
                return k.dma(pool, yT.rearrange("(c p) n -> p c n", p=128)[:, :, (t - 1) * TW:t * TW], x_t[:, :, 0:N], reads=bx, writes=[b_xs[t - 1]])
            if t == 0:
                k.dma(pool, xs_meta[:, :, :], x_t[:, :, 0:N], reads=bx, writes=[b_xs_meta])
            else:
                k.dma(pool, xs[t - 1], x_t[:, :, 0:N], reads=bx, writes=[b_xs[t - 1]])
            return None

        cst2 = din("cst2", [128, 4, NMETA])
        cst2_t = sb("cst2_t", [128, 4, NMETA], F32)
        ld(sp, cst2_t[:], cst2[:, :, :], writes=[b_cst])

        if stage >= 1:
            emit_casts(0)
        if stage >= 2:
            rope_tables()
        out_evs = []
        pending_loads = [None]
        for l in range(n_layers if stage >= 3 else 0):
            if l > 0:
                k.epoch()
            if l + 1 < n_layers:
                emit_casts(l + 1)
            for t in range(n_tiles + 1):
                gi = l * (n_tiles + 1) + t
                if gi == 0:
                    emit_loads(0, 0, 0)
                nxt = (l, t + 1) if t < n_tiles else ((l + 1, 0) if l + 1 < n_layers else None)
                pending_loads[0] = (lambda nxt=nxt, gi=gi: emit_loads(nxt[0], nxt[1], (gi + 1) % 2)) if nxt else None
                ev = tile_layer(l, t, gi % 2)
                if pending_loads[0] is not None:
                    pending_loads[0]()
                    pending_loads[0] = None
                if ev is not None:
                    out_evs.append(ev)
        for ev in out_evs:
            pool.wait(ev)
        for sm in k.all_sems:
            if sm.cnt:
                pool.wait((sm, sm.cnt))
    return nc


def make_consts():
    cst = np.zeros((128, 160), np.float32)
    p = np.arange(128)[:, None]
    c = np.arange(128)[None, :]
    cst[:, 0:128] = (p <= c).astype(np.float32)
    i = np.arange(32) % 16
    cst[0:32, 128] = (10000.0 ** (-(i.astype(np.float32)) / 16.0)).astype(np.float32)
    cst[0:32, 129] = np.where(np.arange(32) < 16, -1.0, 1.0)
    cst[0:32, 130] = np.where(np.arange(32) < 16, math.pi, -math.pi)
    cst[0:32, 131] = -math.pi
    cst[:, 132:148] = np.arange(16, dtype=np.float32)[None, :]
    cst[:, 148] = EPS
    cst2 = np.zeros((128, 4, NMETA), np.float32)
    for g, w in enumerate(POOL_W):
        cst2[:, g, :] = 1.0 / np.minimum(np.arange(1, NMETA + 1), w).astype(np.float32)[None, :]
    return cst, cst2


_CACHE = {}


def run(inputs, n_layers=DEPTH, n_tiles=SEQ // TW, stage=3.7):
    key = (n_layers, n_tiles, stage)
    if key not in _CACHE:
        _CACHE[key] = build(n_layers, n_tiles, stage)
    nc = _CACHE[key]
    S = n_tiles * TW
    cst, cst2 = make_consts()
    x = np.asarray(inputs["x"], np.float32)
    B = x.shape[0]
    shared = {
        "metaT": np.ascontiguousarray(np.asarray(inputs["meta_tokens"], np.float32).T),
        "cst": cst, "cst2": cst2,
    }
    for n in ("norm_gain", "w_in", "pool_w_group", "pool_scale", "pool_w_up", "q_a_norm_gain", "kv_a_norm_gain",
              "w_q_b", "w_kv_b", "q_norm_gain", "k_norm_gain", "mla_w_up", "w_out"):
        shared[n] = np.ascontiguousarray(np.asarray(inputs[n], np.float32)[:n_layers])
    in_maps = []
    for c in range(8):
        b = c % B
        m = dict(shared)
        m["xT"] = np.ascontiguousarray(x[b, :S].T)
        m["pos"] = np.ascontiguousarray(np.asarray(inputs["positions"], np.int32)[b:b + 1, :S])
        in_maps.append(m)
    res = run_bass_kernel_spmd(nc, in_maps, core_ids=list(range(8)))
    out = np.stack([np.ascontiguousarray(res.results[b]["yT"].T) for b in range(B)], axis=0)
    return out.astype(np.float32)


def run_unfused(inputs):
    lpl = LAYERS_PER_LAUNCH
    key = ("unfused", lpl)
    if key not in _CACHE:
        _CACHE[key] = build(lpl, SEQ // TW, meta_out=True)
    nc = _CACHE[key]
    cst, cst2 = make_consts()
    x = np.asarray(inputs["x"], np.float32)
    B = x.shape[0]
    xT = [np.ascontiguousarray(x[b].T) for b in range(B)]
    metaT = [np.ascontiguousarray(np.asarray(inputs["meta_tokens"], np.float32).T) for _ in range(B)]
    pos = np.asarray(inputs["positions"], np.int32)
    names = ("norm_gain", "w_in", "pool_w_group", "pool_scale", "pool_w_up", "q_a_norm_gain", "kv_a_norm_gain",
             "w_q_b", "w_kv_b", "q_norm_gain", "k_norm_gain", "mla_w_up", "w_out")
    for l in range(0, DEPTH, lpl):
        shared = {"cst": cst, "cst2": cst2}
        for n in names:
            shared[n] = np.ascontiguousarray(np.asarray(inputs[n], np.float32)[l:l + lpl])
        in_maps = []
        for c in range(8):
            b = c % B
            m = dict(shared)
            m["xT"] = xT[b]
            m["metaT"] = metaT[b]
            m["pos"] = np.ascontiguousarray(pos[b:b + 1])
            in_maps.append(m)
        res = run_bass_kernel_spmd(nc, in_maps, core_ids=list(range(8)))
        xT = [np.ascontiguousarray(res.results[b]["yT"]) for b in range(B)]
        metaT = [np.ascontiguousarray(res.results[b]["metaT_out"]) for b in range(B)]
    return np.stack([np.ascontiguousarray(xT[b].T) for b in range(B)], axis=0).astype(np.float32)


FUSED = True
LAYERS_PER_LAUNCH = 2


def kernel(**inputs):
    return run(inputs) if FUSED else run_unfused(inputs)
```

```python
import math
from contextlib import ExitStack

import numpy as np
import concourse.bass as bass
import concourse.mybir as mybir
from concourse.bass_utils import run_bass_kernel_spmd

F32 = mybir.dt.float32
BF16 = mybir.dt.bfloat16
I32 = mybir.dt.int32
AF = mybir.ActivationFunctionType
ALU = mybir.AluOpType

D = 1024
DEPTH = 4
BATCH = 4
SEQ = 8192
NMETA = 16
NH = 8
D_IN = 4640
EPS = 1e-6
TW = 512
POOL_W = (2, 4, 8, 16)
WSLOT = 1536
C_U, C_ZP, C_CQ, C_CKV, C_KR, C_ZM, C_GP, C_GM = 0, 512, 1024, 1792, 2048, 2080, 2592, 3616


class Sem:
    def __init__(self, h, owner=None):
        self.h = h
        self.cnt = 0
        self.owner = owner


class Buf:
    __slots__ = ("w", "r", "name")

    def __init__(self, name=""):
        self.w = None
        self.r = {}
        self.name = name


class Eng:
    def __init__(self, k, h, name):
        self.k = k
        self.h = h
        self.name = name
        self.sem = None
        self.known = {}
        self.new_epoch()

    def new_epoch(self):
        self.sem = self.k.new_sem(self.name, owner=self.name)

    def needs(self, ev):
        if ev is None:
            return False
        s, v = ev
        if self.known.get(s, 0) >= v:
            return False
        if self.name == "pe" and s.owner == "pe":
            return False
        return True

    def wait(self, ev):
        if self.needs(ev):
            s, v = ev
            self.h.wait_ge(s.h, v)
            self.known[s] = v

    def pending(self, evs):
        need = {}
        for ev in evs:
            if self.needs(ev):
                s, v = ev
                if need.get(s, 0) < v:
                    need[s] = v
        for s, v in need.items():
            self.known[s] = v
        return list(need.items())

    def issue(self, fn, evs):
        pend = self.pending(evs)
        for s, v in pend[:-1]:
            self.h.wait_ge(s.h, v)
        ins = fn()
        if pend:
            s, v = pend[-1]
            ins._wait_ge(s.h, v)
        return ins


class K:
    def __init__(self, nc, stack):
        self.nc = nc
        self.stack = stack
        self.nsem = 0
        self.all_sems = []
        self.pe = Eng(self, nc.tensor, "pe")
        self.act = Eng(self, nc.scalar, "act")
        self.dve = Eng(self, nc.vector, "dve")
        self.pool = Eng(self, nc.gpsimd, "pool")
        self.sp = Eng(self, nc.sync, "sp")
        self.store_sems = [self.new_sem("st%d" % i) for i in range(8)]
        self.store_i = 0

    def new_sem(self, name, owner=None):
        self.nsem += 1
        sm = Sem(self.stack.enter_context(self.nc.semaphore("%s_%d" % (name, self.nsem))), owner)
        self.all_sems.append(sm)
        return sm

    def epoch(self):
        for e in (self.pe, self.act, self.dve, self.pool, self.sp):
            e.new_epoch()

    def sb(self, name, shape, dt):
        return self.stack.enter_context(self.nc.sbuf_tensor(name, list(shape), dt))

    def ps(self, name):
        return self.stack.enter_context(self.nc.psum_tensor(name, [128, 512], F32))

    @staticmethod
    def _deps(reads, writes):
        evs = []
        for b in reads:
            evs.append(b.w)
        for b in writes:
            evs.append(b.w)
            evs.extend(b.r.items())
        return evs

    @staticmethod
    def _update(ev, reads, writes):
        s, v = ev
        for b in reads:
            if b.r.get(s, 0) < v:
                b.r[s] = v
        for b in writes:
            b.w = ev
            b.r = {}

    def op(self, eng, fn, reads=(), writes=()):
        ins = eng.issue(fn, self._deps(reads, writes))
        eng.sem.cnt += 1
        ins.then_inc(eng.sem.h, 1)
        ev = (eng.sem, eng.sem.cnt)
        self._update(ev, reads, writes)
        return ev

    def dma(self, q, out, in_, reads=(), writes=(), sem=None, extra=(), **kw):
        evs = self._deps(reads, writes) + list(extra)
        if sem is None:
            sem = self.store_sems[self.store_i % len(self.store_sems)]
            self.store_i += 1
            if sem.cnt:
                evs.append((sem, sem.cnt))
        ins = q.issue(lambda: q.h.dma_start(out=out, in_=in_, **kw), evs)
        sem.cnt += 16
        ins.then_inc(sem.h, 16)
        ev = (sem, sem.cnt)
        self._update(ev, reads, writes)
        return ev


class Ring:
    def __init__(self, items):
        self.items = items
        self.i = 0

    def next(self):
        it = self.items[self.i % len(self.items)]
        self.i += 1
        return it


def chunk_plan():
    p = []
    p += [("ckv0", 1024), ("ckv1", 1024), ("kr", 1536), ("kvbk", 1024), ("kvbv", 1024)]
    p += [("cq%d" % j, 1024) for j in range(6)]
    p += [("qb%d" % h, 1152) for h in range(NH)]
    p += [("zm%d" % c, 1024) for c in range(4)]
    p += [("u%d" % g, 1024) for g in range(4)]
    p += [("zp%d" % g, 1024) for g in range(4)]
    p += [("pg", 512)]
    for m in range(8):
        p += [("gp%d" % m, 1024), ("gm%d" % m, 1024), ("up%d" % m, 1024)]
    p += [("wo%d" % m, 1024) for m in range(8)]
    return p


def build(n_layers=DEPTH, n_tiles=SEQ // TW, stage=3.7, meta_out=False):
    nc = bass.Bass("TRN2", target_bir_lowering=False)
    S = n_tiles * TW
    L = NMETA + S
    NKB = 1 + 4 * n_tiles
    plan = chunk_plan()
    cidx = {n: i for i, (n, _) in enumerate(plan)}
    NCH = len(plan)

    def din(name, shape, dt=F32):
        return nc.dram_tensor(name, list(shape), dt, kind="ExternalInput").ap()

    xT = din("xT", [D, S])
    pos_in = din("pos", [1, S], I32)
    metaT = din("metaT", [D, NMETA])
    norm_gain = din("norm_gain", [n_layers, D])
    w_in = din("w_in", [n_layers, D, D_IN])
    pool_w_group = din("pool_w_group", [n_layers, 4, 128, 128])
    pool_scale = din("pool_scale", [n_layers, 512])
    pool_w_up = din("pool_w_up", [n_layers, 512, D])
    q_a_norm_gain = din("q_a_norm_gain", [n_layers, 768])
    kv_a_norm_gain = din("kv_a_norm_gain", [n_layers, 256])
    w_q_b = din("w_q_b", [n_layers, 768, 768])
    w_kv_b = din("w_kv_b", [n_layers, 256, 1024])
    q_norm_gain = din("q_norm_gain", [n_layers, 96])
    k_norm_gain = din("k_norm_gain", [n_layers, 96])
    mla_w_up = din("mla_w_up", [n_layers, 512, D])
    w_out = din("w_out", [n_layers, D, D])
    cst = din("cst", [128, 160])
    yT = nc.dram_tensor("yT", [D, S], F32, kind="ExternalOutput").ap()
    metaT_out = nc.dram_tensor("metaT_out", [D, NMETA], F32, kind="ExternalOutput").ap() if meta_out else None

    def dscr(name, shape, dt):
        return nc.dram_tensor(name, list(shape), dt, kind="Internal").ap()

    wsc = dscr("wsc", [n_layers, NCH, 128, WSLOT], BF16)
    xs_meta = dscr("xs_meta", [128, 8, NMETA], F32)
    xs = dscr("xs", [n_tiles, 128, 8, TW], F32)
    kT_d = dscr("kT_d", [NKB, 96, NH, 128], BF16)
    v_d = dscr("v_d", [NKB, 128, NH, 128], BF16)
    cos_d = dscr("cos_d", [32, L], F32)
    sin_d = dscr("sin_d", [32, L], F32)

    with ExitStack() as stack:
        k = K(nc, stack)
        pe, act, dve, pool, sp = k.pe, k.act, k.dve, k.pool, k.sp
        sb = k.sb

        cst_t = sb("cst_t", [128, 160], F32)
        ones_bf = sb("ones_bf", [128, 128], BF16)
        tri_bf = sb("tri_bf", [128, 128], BF16)
        gn_t = sb("gn_t", [128, n_layers, 8], F32)
        gqa_t = sb("gqa_t", [128, n_layers, 6], F32)
        gkva_t = sb("gkva_t", [128, n_layers, 2], F32)
        psc_t = sb("psc_t", [128, n_layers, 4], F32)
        gq_t = sb("gq_t", [96, n_layers, 2], F32)
        gk_t = sb("gk_t", [96, n_layers, 2], F32)
        xt = [sb("xt%d" % i, [128, 8, TW], F32) for i in range(2)]
        hT = sb("hT", [128, 8, TW], BF16)
        sq = [sb("sq%d" % i, [128, TW], BF16) for i in range(2)]
        sqk = [sb("sqk%d" % i, [96, TW], BF16) for i in range(2)]
        rs = [sb("rs%d" % i, [128, TW], F32) for i in range(3)]
        ckv_raw = sb("ckv_raw", [128, 2, TW], F32)
        ckvn = sb("ckvn", [128, 2, TW], BF16)
        kr_raw = sb("kr_raw", [96, 2, TW], F32)
        kR = sb("kR", [96, TW], F32)
        tmp32 = [sb("tmp32_%d" % i, [96, TW], F32) for i in range(2)]
        cs_t2 = [sb("cs_t%d" % i, [96, 2, TW], F32) for i in range(2)]
        cg_t = sb("cg_t", [96, 4, TW], F32)
        kT_t = sb("kT_t", [96, 4, NH, 128], BF16)
        va_t = sb("va_t", [128, 4, NH, 128], BF16)
        cqn = sb("cqn", [128, 6, TW], BF16)
        qT = sb("qT", [96, NH, TW], BF16)
        cq_raw = sb("cq_raw", [128, 6, TW], F32)
        zm = sb("zm", [128, 4, TW], F32)
        zp = cq_raw[:, 0:4, :]
        u_t = sb("u_t", [128, 4, 16 + TW], F32)
        pt = [sb("pt%d" % i, [128, 16 + TW], F32) for i in range(2)]
        pg = sb("pg", [128, 4, TW], BF16)
        om = sb("om", [128, 4, TW], BF16)
        rec = [sb("rec%d" % i, [128, TW], F32) for i in range(2)]
        otmp = [sb("otmp%d" % i, [128, TW], F32) for i in range(1)] * 2
        sg = [sb("sg%d" % i, [128, TW], F32) for i in range(4)]
        mt = [sb("mt%d" % i, [128, TW], F32) for i in range(2)]
        merged = sb("merged", [128, 8, TW], BF16)
        pbuf = [sb("pbuf%d" % i, [128, TW], BF16) for i in range(4)]
        mixed = pbuf
        posf = rs[0][0:32]
        ang = [rs[1][0:32], rs[2][0:32]]
        nq = tmp32[0][0:32]
        posi = tmp32[1][0:32].bitcast(I32)
        NW = 5
        wring = [sb("wr%d" % i, [128, WSLOT], BF16) for i in range(NW)]
        NKV = 4
        kslot = [sb("ks%d" % i, [96, 4, 128], BF16) for i in range(NKV)]
        vslot = [sb("vs%d" % i, [128, 4, 128], BF16) for i in range(NKV)]
        banks = [k.ps("bank%d" % i) for i in range(8)]

        def bl(name, n):
            return [Buf("%s%d" % (name, i)) for i in range(n)]

        b_cst, b_ones, b_tri, b_gains = Buf(), Buf(), Buf(), Buf()
        b_xt = [bl("xt", 8), bl("xt", 8)]
        b_hT = bl("hT", 8)
        b_sq, b_sqk = bl("sq", 2), bl("sqk", 2)
        b_sqk_rope = bl("sqkr", 2)
        b_rs = bl("rs", 3)
        b_ckv_raw, b_ckvn = bl("ckvr", 2), bl("ckvn", 2)
        b_kr_raw, b_kR = Buf(), Buf()
        b_tmp32 = bl("tmp32", 2)
        b_cs2, b_cg = [Buf(), Buf()], Buf()
        b_kT, b_va = Buf(), Buf()
        b_va_ones = Buf()
        b_cq_raw, b_cqn = bl("cqr", 6), bl("cqn", 6)
        b_qT = bl("qT", NH)
        b_zm = bl("zm", 4)
        b_zp = b_cq_raw[0:4]
        b_u, b_uh = bl("u", 4), bl("uh", 4)
        b_pt = bl("pt", 2)
        b_pg, b_om = bl("pg", 4), bl("om", 4)
        b_rec, b_otmp = bl("rec", 2), bl("otmp", 1) * 2
        b_sg, b_mt = bl("sg", 4), bl("mt", 2)
        b_merged = bl("merged", 8)
        b_pbuf = bl("pbuf", 4)
        b_mixed = b_pbuf
        b_wring, b_kslot, b_vslot = bl("wr", NW), bl("ks", NKV), bl("vs", NKV)
        b_banks = bl("bank", 8)
        b_posi, b_posf, b_ang = b_tmp32[1], b_rs[0], [b_rs[1], b_rs[2]]
        b_nq = b_tmp32[0]
        b_wsc = [[Buf() for _ in range(NCH)] for _ in range(n_layers)]
        b_xs_meta = Buf()
        b_xs = bl("xs", n_tiles)
        b_kvd = bl("kvd", n_tiles + 1)
        b_csd = bl("csd", n_tiles + 1)
        wsem = [k.new_sem("w") for _ in range(NW)]
        ksem = [k.new_sem("ks") for _ in range(NKV)]
        vsem = [k.new_sem("vs") for _ in range(NKV)]
        ld_rings = {"sp": Ring([k.new_sem("ld") for _ in range(4)]), "pool": Ring([k.new_sem("lq") for _ in range(2)])}

        def ld(q, out, in_, reads=(), writes=(), **kw):
            s = ld_rings[q.name].next()
            return k.dma(q, out, in_, reads=reads, writes=writes, sem=s, extra=[(s, s.cnt)] if s.cnt else [], **kw)

        ld(sp, cst_t[:], cst[:, :], writes=[b_cst])
        k.op(dve, lambda: nc.vector.memset(ones_bf[:], 1.0), writes=[b_ones])
        k.op(dve, lambda: nc.vector.tensor_copy(out=tri_bf[:], in_=cst_t[:, 0:128]), reads=[b_cst], writes=[b_tri])
        k.op(dve, lambda: nc.vector.memset(va_t[:], 1.0), writes=[b_va, b_va_ones])
        k.op(dve, lambda: nc.vector.memset(kT_t[:], 0.0), writes=[b_kT])
        for i_ in range(2):
            k.op(dve, lambda i_=i_: nc.vector.memset(pt[i_][:], 0.0), writes=[b_pt[i_]])
        for g in range(4):
            k.op(dve, lambda g=g: nc.vector.memset(u_t[:, g, 0:16], 0.0), writes=[b_uh[g]])

        def colload(dst, src2d, nchunk):
            ld(pool, dst, src2d.rearrange("l (c p o) -> p l c o", p=128, o=1)[:, :, :, 0], writes=[b_gains],
               allow_slow_non_contiguous=True)

        colload(gn_t[:], norm_gain, 8)
        colload(gqa_t[:], q_a_norm_gain, 6)
        colload(gkva_t[:], kv_a_norm_gain, 2)
        colload(psc_t[:], pool_scale, 4)
        for (gt, src) in ((gq_t, q_norm_gain), (gk_t, k_norm_gain)):
            srcT = src.rearrange("l (d o) -> d l o", o=1)[:, :, 0]
            ld(pool, gt[0:96, :, 0], srcT[0:96, :], writes=[b_gains], allow_slow_non_contiguous=True)
            ld(pool, gt[64:80, :, 1], srcT[80:96, :], writes=[b_gains], allow_slow_non_contiguous=True)
            ld(pool, gt[80:96, :, 1], srcT[64:80, :], writes=[b_gains], allow_slow_non_contiguous=True)

        def cast(l, name, col0, out_view, in_view):
            ci = cidx[name]
            k.dma(pool, out_view(wsc[l, ci]), in_view, writes=[b_wsc[l][ci]])

        def emit_casts(l):
            def std(name, w2d, c0, kc, m, off=0, mtot=None):
                mtot_ = mtot or m
                ci = cidx[name]
                o = wsc[l, ci][:, 0:kc * mtot_].rearrange("p (c m) -> p c m", m=mtot_)[:, :, off:off + m]
                i = w2d.rearrange("(c p) n -> p c n", p=128)[:, :, c0:c0 + m]
                k.dma(pool, o, i, writes=[b_wsc[l][ci]])

            wi = w_in[l]
            std("ckv0", wi, C_CKV, 8, 128)
            std("ckv1", wi, C_CKV + 128, 8, 128)
            std("kr", wi, C_CKV, 8, 64, off=0, mtot=192)
            std("kr", wi, C_KR, 8, 32, off=64, mtot=192)
            std("kr", wi, C_CKV, 8, 64, off=96, mtot=192)
            std("kr", wi, C_KR + 16, 8, 16, off=160, mtot=192)
            std("kr", wi, C_KR, 8, 16, off=176, mtot=192)
            wkv = w_kv_b[l]
            for h in range(NH):
                ci = cidx["kvbk"]
                ov = wsc[l, ci][:, 0:2 * NH * 64].rearrange("p (c h m) -> p c h m", h=NH, m=64)
                iv = wkv.rearrange("(c p) n -> p c n", p=128)
                k.dma(pool, ov[:, :, h, :], iv[:, :, h * 128:h * 128 + 64], writes=[b_wsc[l][ci]])
                ci = cidx["kvbv"]
                ov = wsc[l, ci][:, 0:2 * 512].rearrange("p (c h m) -> p c h m", h=NH, m=64)
                k.dma(pool, ov[:, :, h, :], iv[:, :, h * 128 + 64:h * 128 + 128], writes=[b_wsc[l][ci]])
            for j in range(6):
                std("cq%d" % j, wi, C_CQ + 128 * j, 8, 128)
            wq = w_q_b[l]
            for h in range(NH):
                n = "qb%d" % h
                std(n, wq, h * 96, 6, 96, off=0, mtot=192)
                std(n, wq, h * 96, 6, 64, off=96, mtot=192)
                std(n, wq, h * 96 + 80, 6, 16, off=160, mtot=192)
                std(n, wq, h * 96 + 64, 6, 16, off=176, mtot=192)
            for c in range(4):
                std("zm%d" % c, wi, C_ZM + 128 * c, 8, 128)
            for g in range(4):
                std("u%d" % g, wi, C_U + 128 * g, 8, 128)
                std("zp%d" % g, wi, C_ZP + 128 * g, 8, 128)
            ci = cidx["pg"]
            k.dma(pool, wsc[l, ci][:, 0:512].rearrange("p (g m) -> p g m", m=128),
                  pool_w_group[l].rearrange("g p m -> p g m"), writes=[b_wsc[l][ci]])
            for m in range(8):
                std("gp%d" % m, wi, C_GP + 128 * m, 8, 128)
                std("gm%d" % m, wi, C_GM + 128 * m, 8, 128)
                ci = cidx["up%d" % m]
                ov = wsc[l, ci][:, 0:1024].rearrange("p (t c m) -> p t c m", t=2, m=128)
                k.dma(pool, ov[:, 0], pool_w_up[l].rearrange("(c p) n -> p c n", p=128)[:, :, 128 * m:128 * m + 128],
                      writes=[b_wsc[l][ci]])
                k.dma(pool, ov[:, 1], mla_w_up[l].rearrange("(c p) n -> p c n", p=128)[:, :, 128 * m:128 * m + 128],
                      writes=[b_wsc[l][ci]])
            for m in range(8):
                std("wo%d" % m, w_out[l], 128 * m, 8, 128)

        wstate = {"i": 0}

        def wload(l, name):
            ci = cidx[name]
            n = plan[ci][1]
            s = wstate["i"] % NW
            wstate["i"] += 1
            k.dma(sp, wring[s][:, 0:n], wsc[l, ci][:, 0:n], reads=[b_wsc[l][ci]], writes=[b_wring[s]], sem=wsem[s])
            return wring[s], b_wring[s]

        def rope_tables():
            invf = cst_t[0:32, 128:129]
            sgn = cst_t[0:32, 129:130]
            for t in range(n_tiles + 1):
                N = NMETA if t == 0 else TW
                t0 = 0 if t == 0 else NMETA + (t - 1) * TW
                if t == 0:
                    k.op(dve, lambda: nc.vector.tensor_copy(out=posf[:, 0:N], in_=cst_t[0:32, 132:148]),
                         reads=[b_cst], writes=[b_posf])
                else:
                    ld(sp, posi[:, 0:N], pos_in[0, (t - 1) * TW:t * TW].partition_broadcast(32), writes=[b_posi])
                    k.op(dve, lambda: nc.vector.tensor_copy(out=posf[:, 0:N], in_=posi[:, 0:N]),
                         reads=[b_posi], writes=[b_posf])
                    k.op(dve, lambda: nc.vector.tensor_scalar_add(out=posf[:, 0:N], in0=posf[:, 0:N], scalar1=float(NMETA)),
                         reads=[b_posf], writes=[b_posf])
                k.op(dve, lambda: nc.vector.tensor_scalar_mul(out=posf[:, 0:N], in0=posf[:, 0:N], scalar1=invf),
                     reads=[b_posf, b_cst], writes=[b_posf])
                C1 = 6.28125
                C2 = 2.0 * math.pi - C1
                for which, shift in ((0, math.pi * 0.5), (1, 0.0)):
                    a = ang[which]
                    ba = b_ang[which]
                    k.op(dve, lambda a=a, shift=shift: nc.vector.tensor_scalar_add(out=a[:, 0:N], in0=posf[:, 0:N], scalar1=shift),
                         reads=[b_posf], writes=[ba])
                    k.op(dve, lambda a=a: nc.vector.tensor_scalar_mul(out=nq[:, 0:N], in0=a[:, 0:N], scalar1=1.0 / (2.0 * math.pi)),
                         reads=[ba], writes=[b_nq])
                    k.op(dve, lambda: nc.vector.tensor_copy(out=posi[:, 0:N], in_=nq[:, 0:N]), reads=[b_nq], writes=[b_posi])
                    k.op(dve, lambda: nc.vector.tensor_copy(out=nq[:, 0:N], in_=posi[:, 0:N]), reads=[b_posi], writes=[b_nq])
                    k.op(dve, lambda a=a: nc.vector.scalar_tensor_tensor(out=a[:, 0:N], in0=nq[:, 0:N], scalar=-C1, in1=a[:, 0:N], op0=ALU.mult, op1=ALU.add),
                         reads=[b_nq, ba], writes=[ba])
                    k.op(dve, lambda a=a: nc.vector.scalar_tensor_tensor(out=a[:, 0:N], in0=nq[:, 0:N], scalar=-C2, in1=a[:, 0:N], op0=ALU.mult, op1=ALU.add),
                         reads=[b_nq, ba], writes=[ba])
                    k.op(dve, lambda a=a: nc.vector.tensor_single_scalar(out=nq[:, 0:N], in_=a[:, 0:N], scalar=math.pi, op=ALU.is_gt),
                         reads=[ba], writes=[b_nq])
                    k.op(dve, lambda a=a: nc.vector.scalar_tensor_tensor(out=a[:, 0:N], in0=nq[:, 0:N], scalar=-2.0 * math.pi, in1=a[:, 0:N], op0=ALU.mult, op1=ALU.add),
                         reads=[b_nq, ba], writes=[ba])
                    k.op(dve, lambda a=a: nc.vector.tensor_scalar_max(out=a[:, 0:N], in0=a[:, 0:N], scalar1=-3.1415925), reads=[ba], writes=[ba])
                    k.op(dve, lambda a=a: nc.vector.tensor_scalar_min(out=a[:, 0:N], in0=a[:, 0:N], scalar1=3.1415925), reads=[ba], writes=[ba])
                    if which == 0:
                        k.op(act, lambda a=a: nc.scalar.activation(out=a[:, 0:N], in_=a[:, 0:N], func=AF.Sin),
                             reads=[ba], writes=[ba])
                        k.dma(pool, cos_d[:, t0:t0 + N], a[:, 0:N], reads=[ba], writes=[b_csd[t]])
                    else:
                        k.op(act, lambda a=a: nc.scalar.activation(out=a[:, 0:N], in_=a[:, 0:N], func=AF.Sin, scale=sgn),
                             reads=[ba, b_cst], writes=[ba])
                        k.dma(pool, sin_d[:, t0:t0 + N], a[:, 0:N], reads=[ba], writes=[b_csd[t]])

        gring = Ring([0, 1, 2, 3, 4, 5, 6])

        def gbank():
            i = gring.next()
            return banks[i], b_banks[i]

        def proj(l, name, M, N, rhs_list, rhs_bufs, coloff=0, mtot=None, kc=None):
            w, bw = wload(l, name)
            kc = kc or len(rhs_list)
            mt_ = mtot or M
            wv = w[:, 0:kc * mt_].rearrange("p (c m) -> p c m", m=mt_)
            bank, bb = gbank()
            for c in range(kc):
                k.op(pe, lambda c=c: nc.tensor.matmul(bank[0:M, 0:N], lhsT=wv[:, c, coloff:coloff + M], rhs=rhs_list[c],
                                                       start=(c == 0), stop=(c == kc - 1)),
                     reads=[bw, rhs_bufs[c]], writes=[bb])
            return bank, bb, wv, bw

        def rstd_from(bank, bb, P, N, inv_n, dst, bdst):
            k.op(act, lambda: nc.scalar.activation(out=dst[0:P, 0:N], in_=bank[0:P, 0:N], func=AF.Ln, bias=cst_t[0:P, 148:149], scale=inv_n),
                 reads=[bb, b_cst], writes=[bdst])
            k.op(act, lambda: nc.scalar.activation(out=dst[0:P, 0:N], in_=dst[0:P, 0:N], func=AF.Exp, scale=-0.5), reads=[bdst], writes=[bdst])

        sq_ring = Ring([0, 1])
        sqk_ring = Ring([0, 1])
        rs_ring = Ring([0, 1, 2])
        raw_ring = Ring([0, 1])

        def emit_loads(l, t, xb):
            N = NMETA if t == 0 else TW
            x_t, bx = xt[xb], b_xt[xb]
            if l == 0:
                src = metaT.rearrange("(c p) n -> p c n", p=128) if t == 0 else \
                    xT.rearrange("(c p) n -> p c n", p=128)[:, :, (t - 1) * TW:t * TW]
                ld(sp, x_t[:, :, 0:N], src, writes=bx)
            else:
                if t == 0:
                    ld(sp, x_t[:, :, 0:N], xs_meta[:, :, :], reads=[b_xs_meta], writes=bx)
                else:
                    ld(sp, x_t[:, :, 0:N], xs[t - 1], reads=[b_xs[t - 1]], writes=bx)
            t0 = 0 if t == 0 else NMETA + (t - 1) * TW
            ld(sp, cs_t2[xb][64:96, 0, 0:N], cos_d[:, t0:t0 + N], reads=[b_csd[t]], writes=[b_cs2[xb]])
            ld(sp, cs_t2[xb][64:96, 1, 0:N], sin_d[:, t0:t0 + N], reads=[b_csd[t]], writes=[b_cs2[xb]])

        def tile_layer(l, t, xb):
            N = NMETA if t == 0 else TW
            NB = 1 if t == 0 else 4
            last = (l == n_layers - 1)
            x_t, bx = xt[xb], b_xt[xb]
            cs_t, b_cs = cs_t2[xb], b_cs2[xb]
            t0 = 0 if t == 0 else NMETA + (t - 1) * TW
            for j, (gt, col, tab) in enumerate(((gq_t, 0, 0), (gq_t, 1, 1), (gk_t, 0, 0), (gk_t, 1, 1))):
                k.op(dve, lambda j=j, gt=gt, col=col, tab=tab: nc.vector.tensor_scalar_mul(
                    out=cg_t[64:96, j, 0:N], in0=cs_t[64:96, tab, 0:N], scalar1=gt[64:96, l, col:col + 1]),
                    reads=[b_cs, b_gains], writes=[b_cg])

            bank, bb = gbank()
            for c in range(8):
                i = sq_ring.next()
                k.op(act, lambda c=c, i=i: nc.scalar.activation(out=sq[i][:, 0:N], in_=x_t[:, c, 0:N], func=AF.Square),
                     reads=[bx[c]], writes=[b_sq[i]])
                k.op(pe, lambda c=c, i=i: nc.tensor.matmul(bank[:, 0:N], lhsT=ones_bf[:, :], rhs=sq[i][:, 0:N], start=(c == 0), stop=(c == 7)),
                     reads=[b_ones, b_sq[i]], writes=[bb])
            r = rs_ring.next()
            rstd_from(bank, bb, 128, N, 1.0 / D, rs[r], b_rs[r])
            for c in range(8):
                k.op(dve, lambda c=c: nc.vector.scalar_tensor_tensor(out=hT[:, c, 0:N], in0=x_t[:, c, 0:N], scalar=gn_t[:, l, c:c + 1],
                                                                     in1=rs[r][:, 0:N], op0=ALU.mult, op1=ALU.mult),
                     reads=[bx[c], b_rs[r], b_gains], writes=[b_hT[c]])
            h_rhs = [hT[:, c, 0:N] for c in range(8)]

            if stage < 3.1005:
                return None
            ssb, bssb = banks[7], b_banks[7]
            if stage < 3.1015:
                wload(l, "ckv0")
                return None
            if stage < 3.1025:
                proj(l, "ckv0", 128, N, h_rhs, b_hT)
                return None
            sspend = None
            for j in range(2):
                bank, bb, _, _ = proj(l, "ckv%d" % j, 128, N, h_rhs, b_hT)
                k.op(dve, lambda j=j, bank=bank: nc.vector.tensor_copy(out=ckv_raw[:, j, 0:N], in_=bank[:, 0:N]),
                     reads=[bb], writes=[b_ckv_raw[j]])
                if stage < 3.1035:
                    if j == 1:
                        return None
                    continue
                i = sq_ring.next()
                k.op(act, lambda i=i, j=j: nc.scalar.activation(out=sq[i][:, 0:N], in_=ckv_raw[:, j, 0:N], func=AF.Square),
                     reads=[b_ckv_raw[j]], writes=[b_sq[i]])

                def sstail(j=j, i=i):
                    k.op(pe, lambda: nc.tensor.matmul(ssb[:, 0:N], lhsT=ones_bf[:, :], rhs=sq[i][:, 0:N], start=(j == 0), stop=(j == 1)),
                         reads=[b_ones, b_sq[i]], writes=[bssb])
                if sspend is not None:
                    sspend()
                sspend = sstail
            if sspend is not None:
                sspend()
            if stage < 3.1045:
                return None
            r = rs_ring.next()
            rstd_from(ssb, bssb, 128, N, 1.0 / 256, rs[r], b_rs[r])
            for j in range(2):
                k.op(dve, lambda j=j: nc.vector.scalar_tensor_tensor(out=ckvn[:, j, 0:N], in0=ckv_raw[:, j, 0:N], scalar=gkva_t[:, l, j:j + 1],
                                                                     in1=rs[r][:, 0:N], op0=ALU.mult, op1=ALU.mult),
                     reads=[b_ckv_raw[j], b_rs[r], b_gains], writes=[b_ckvn[j]])
            if stage < 3.1105:
                return None
            w, bw = wload(l, "kr")
            wv = w[:, 0:1536].rearrange("p (c m) -> p c m", m=192)
            for v_ in range(2):
                bank, bb = gbank()
                for c in range(8):
                    k.op(pe, lambda c=c, v_=v_, bank=bank: nc.tensor.matmul(bank[0:96, 0:N], lhsT=wv[:, c, 96 * v_:96 * v_ + 96], rhs=h_rhs[c],
                                                                         start=(c == 0), stop=(c == 7)),
                         reads=[bw, b_hT[c]], writes=[bb])
                k.op(dve, lambda v_=v_, bank=bank: nc.vector.tensor_copy(out=kr_raw[64:96, v_, 0:N], in_=bank[64:96, 0:N]),
                     reads=[bb], writes=[b_kr_raw])
                if v_ == 0:
                    for i in range(2):
                        k.op(act, lambda i=i: nc.scalar.activation(out=sqk[i][64:96, 0:N], in_=kr_raw[64:96, 0, 0:N], func=AF.Square),
                             reads=[b_kr_raw], writes=[b_sqk_rope[i]])
            k.op(dve, lambda: nc.vector.tensor_tensor(out=tmp32[0][64:96, 0:N], in0=kr_raw[64:96, 0, 0:N], in1=cg_t[64:96, 2, 0:N], op=ALU.mult),
                 reads=[b_kr_raw, b_cg], writes=[b_tmp32[0]])
            k.op(dve, lambda: nc.vector.tensor_tensor(out=kR[64:96, 0:N], in0=kr_raw[64:96, 1, 0:N], in1=cg_t[64:96, 3, 0:N], op=ALU.mult),
                 reads=[b_kr_raw, b_cg], writes=[b_kR])
            k.op(dve, lambda: nc.vector.tensor_tensor(out=kR[64:96, 0:N], in0=kR[64:96, 0:N], in1=tmp32[0][64:96, 0:N], op=ALU.add),
                 reads=[b_kR, b_tmp32[0]], writes=[b_kR])
            if stage < 3.1205:
                return None
            w, bw = wload(l, "kvbk")
            wkk = w[:, 0:2 * NH * 64].rearrange("p (c h m) -> p c h m", h=NH, m=64)
            ckv_rhs = [ckvn[:, c, 0:N] for c in range(2)]

            def ktv(rows, h):
                if t == 0:
                    return kT_t[rows, 0, h, 0:N]
                return kT_t[rows, :, h, :]

            def as_blk(ap):
                return ap if t == 0 else ap.rearrange("p (b c) -> p b c", c=128)

            kpend = None
            for h in range(NH):
                bank, bb = gbank()
                for c in range(2):
                    k.op(pe, lambda c=c, h=h, bank=bank: nc.tensor.matmul(bank[0:64, 0:N], lhsT=wkk[:, c, h, :], rhs=ckv_rhs[c], start=(c == 0), stop=(c == 1)),
                         reads=[bw, b_ckvn[c]], writes=[bb])
                ri_ = raw_ring.next()
                k.op(dve, lambda ri_=ri_, bank=bank: nc.vector.tensor_copy(out=rec[ri_][0:64, 0:N], in_=bank[0:64, 0:N]), reads=[bb], writes=[b_rec[ri_]])
                i = sqk_ring.next()
                k.op(act, lambda i=i, ri_=ri_: nc.scalar.activation(out=sqk[i][0:64, 0:N], in_=rec[ri_][0:64, 0:N], func=AF.Square),
                     reads=[b_rec[ri_]], writes=[b_sqk[i]])

                def ktail(h=h, ri_=ri_, i=i):
                    bank2, bb2 = gbank()
                    k.op(pe, lambda: nc.tensor.matmul(bank2[0:96, 0:N], lhsT=ones_bf[0:96, 0:96], rhs=sqk[i][0:96, 0:N], start=True, stop=True),
                         reads=[b_ones, b_sqk[i], b_sqk_rope[i]], writes=[bb2])
                    r = rs_ring.next()
                    rstd_from(bank2, bb2, 96, N, 1.0 / 96, rs[r], b_rs[r])
                    k.op(dve, lambda: nc.vector.scalar_tensor_tensor(
                        out=ktv(slice(0, 64), h), in0=as_blk(rec[ri_][0:64, 0:N]), scalar=gk_t[0:64, l, 0:1],
                        in1=as_blk(rs[r][0:64, 0:N]), op0=ALU.mult, op1=ALU.mult),
                        reads=[b_rec[ri_], b_rs[r], b_gains], writes=[b_kT])
                    k.op(pool, lambda: nc.gpsimd.tensor_tensor(out=ktv(slice(64, 96), h), in0=as_blk(kR[64:96, 0:N]), in1=as_blk(rs[r][64:96, 0:N]), op=ALU.mult),
                         reads=[b_kR, b_rs[r]], writes=[b_kT])
                if kpend is not None:
                    kpend()
                kpend = ktail
            kpend()
            if stage < 3.1305:
                return None
            w, bw = wload(l, "kvbv")
            wvv = w[:, 0:1024].rearrange("p (c n) -> p c n", n=512)
            for tb in range(NB):
                nt = N if t == 0 else 128
                bank, bb = gbank()
                for c in range(2):
                    k.op(pe, lambda c=c, tb=tb, bank=bank: nc.tensor.matmul(bank[0:nt, 0:512], lhsT=ckvn[:, c, tb * 128:tb * 128 + nt], rhs=wvv[:, c, :],
                                                                         start=(c == 0), stop=(c == 1)),
                         reads=[bw, b_ckvn[c]], writes=[bb])
                pv = bank[0:nt, 0:512].rearrange("p (h two d) -> p h two d", two=2, d=64)
                vv = va_t[0:nt, tb].rearrange("p (h two) c -> p h two c", two=2)
                k.op(act, lambda pv=pv, vv=vv: nc.scalar.copy(out=vv[:, :, 0, 0:64], in_=pv[:, :, 0, :]), reads=[bb, b_va_ones], writes=[b_va])
                k.op(dve, lambda pv=pv, vv=vv: nc.vector.tensor_copy(out=vv[:, :, 1, 64:128], in_=pv[:, :, 1, :]), reads=[bb, b_va_ones], writes=[b_va])
            if stage < 3.1405:
                return None
            n_st = 4 if stage >= 3.2 else int(round((stage - 3.14) * 1000))
            if t == 0:
                if n_st >= 1:
                    k.dma(pool, kT_d[0].rearrange("p h c -> p (h c)"), kT_t[:, 0].rearrange("p h c -> p (h c)"), reads=[b_kT], writes=[b_kvd[0]])
                if n_st >= 2:
                    k.dma(pool, v_d[0][0:N].rearrange("p h c -> p (h c)"), va_t[0:N, 0].rearrange("p h c -> p (h c)"), reads=[b_va], writes=[b_kvd[0]])
            else:
                kb0 = 1 + 4 * (t - 1)
                if n_st >= 3:
                    k.dma(pool, kT_d[kb0:kb0 + 4].rearrange("b p h c -> p b (h c)"), kT_t[:].rearrange("p b h c -> p b (h c)"),
                          reads=[b_kT], writes=[b_kvd[t]])
                if n_st >= 4:
                    k.dma(pool, v_d[kb0:kb0 + 4].rearrange("b p h c -> p b (h c)"), va_t[:].rearrange("p b h c -> p b (h c)"),
                          reads=[b_va], writes=[b_kvd[t]])

            if stage < 3.25:
                return None
            qspend = None
            for j in range(6):
                bank, bb, _, _ = proj(l, "cq%d" % j, 128, N, h_rhs, b_hT)
                k.op(dve, lambda j=j, bank=bank: nc.vector.tensor_copy(out=cq_raw[:, j, 0:N], in_=bank[:, 0:N]), reads=[bb], writes=[b_cq_raw[j]])
                i = sq_ring.next()
                k.op(act, lambda i=i, j=j: nc.scalar.activation(out=sq[i][:, 0:N], in_=cq_raw[:, j, 0:N], func=AF.Square), reads=[b_cq_raw[j]], writes=[b_sq[i]])

                def qstail(j=j, i=i):
                    k.op(pe, lambda: nc.tensor.matmul(ssb[:, 0:N], lhsT=ones_bf[:, :], rhs=sq[i][:, 0:N], start=(j == 0), stop=(j == 5)),
                         reads=[b_ones, b_sq[i]], writes=[bssb])
                if qspend is not None:
                    qspend()
                qspend = qstail
            qspend()
            r = rs_ring.next()
            rstd_from(ssb, bssb, 128, N, 1.0 / 768, rs[r], b_rs[r])
            for j in range(6):
                k.op(dve, lambda j=j: nc.vector.scalar_tensor_tensor(out=cqn[:, j, 0:N], in0=cq_raw[:, j, 0:N], scalar=gqa_t[:, l, j:j + 1],
                                                                     in1=rs[r][:, 0:N], op0=ALU.mult, op1=ALU.mult),
                     reads=[b_cq_raw[j], b_rs[r], b_gains], writes=[b_cqn[j]])
            cq_rhs = [cqn[:, c, 0:N] for c in range(6)]
            qpend = None
            for h in range(NH):
                w, bw = wload(l, "qb%d" % h)
                wq_ = w[:, 0:1152].rearrange("p (c m) -> p c m", m=192)
                bq, bbq = gbank()
                for c in range(6):
                    k.op(pe, lambda c=c, bq=bq: nc.tensor.matmul(bq[0:96, 0:N], lhsT=wq_[:, c, 0:96], rhs=cq_rhs[c], start=(c == 0), stop=(c == 5)),
                         reads=[bw, b_cqn[c]], writes=[bbq])
                bs_, bbs = gbank()
                for c in range(6):
                    k.op(pe, lambda c=c, bs_=bs_: nc.tensor.matmul(bs_[0:96, 0:N], lhsT=wq_[:, c, 96:192], rhs=cq_rhs[c], start=(c == 0), stop=(c == 5)),
                         reads=[bw, b_cqn[c]], writes=[bbs])
                ri_ = raw_ring.next()
                k.op(dve, lambda ri_=ri_, bq=bq: nc.vector.tensor_copy(out=rec[ri_][0:96, 0:N], in_=bq[0:96, 0:N]), reads=[bbq], writes=[b_rec[ri_]])
                i = sqk_ring.next()
                k.op(act, lambda i=i, ri_=ri_: nc.scalar.activation(out=sqk[i][0:96, 0:N], in_=rec[ri_][0:96, 0:N], func=AF.Square),
                     reads=[b_rec[ri_]], writes=[b_sqk[i], b_sqk_rope[i]])
                def qtail(h=h, i=i, ri_=ri_, bq=bq, bbq=bbq, bs_=bs_, bbs=bbs):
                    b3, bb3 = gbank()
                    k.op(pe, lambda i=i, b3=b3: nc.tensor.matmul(b3[0:96, 0:N], lhsT=ones_bf[0:96, 0:96], rhs=sqk[i][0:96, 0:N], start=True, stop=True),
                         reads=[b_ones, b_sqk[i], b_sqk_rope[i]], writes=[bb3])
                    r = rs_ring.next()
                    rstd_from(b3, bb3, 96, N, 1.0 / 96, rs[r], b_rs[r])
                    k.op(dve, lambda h=h, ri_=ri_, r=r: nc.vector.scalar_tensor_tensor(out=qT[0:64, h, 0:N], in0=rec[ri_][0:64, 0:N], scalar=gq_t[0:64, l, 0:1],
                                                                                    in1=rs[r][0:64, 0:N], op0=ALU.mult, op1=ALU.mult),
                         reads=[b_rec[ri_], b_rs[r], b_gains], writes=[b_qT[h]])
                    k.op(pool, lambda ri_=ri_: nc.gpsimd.tensor_tensor(out=tmp32[0][64:96, 0:N], in0=rec[ri_][64:96, 0:N], in1=cg_t[64:96, 0, 0:N], op=ALU.mult),
                         reads=[b_rec[ri_], b_cg], writes=[b_tmp32[0]])
                    k.op(dve, lambda bs_=bs_: nc.vector.tensor_tensor(out=tmp32[1][64:96, 0:N], in0=bs_[64:96, 0:N], in1=cg_t[64:96, 1, 0:N], op=ALU.mult),
                         reads=[bbs, b_cg], writes=[b_tmp32[1]])
                    k.op(pool, lambda: nc.gpsimd.tensor_tensor(out=tmp32[0][64:96, 0:N], in0=tmp32[0][64:96, 0:N], in1=tmp32[1][64:96, 0:N], op=ALU.add),
                         reads=[b_tmp32[0], b_tmp32[1]], writes=[b_tmp32[0]])
                    k.op(pool, lambda h=h, r=r: nc.gpsimd.tensor_tensor(out=qT[64:96, h, 0:N], in0=tmp32[0][64:96, 0:N], in1=rs[r][64:96, 0:N], op=ALU.mult),
                         reads=[b_tmp32[0], b_rs[r]], writes=[b_qT[h]])
                if qpend is not None:
                    qpend()
                qpend = qtail
            qpend()
            for c in range(4):
                bank, bb, _, _ = proj(l, "zm%d" % c, 128, N, h_rhs, b_hT)
                k.op(act, lambda c=c, bank=bank: nc.scalar.activation(out=zm[:, c, 0:N], in_=bank[:, 0:N], func=AF.Silu), reads=[bb], writes=[b_zm[c]])

            if stage < 3.35:
                return None
            if pending_loads[0] is not None:
                pending_loads[0]()
                pending_loads[0] = None
            scale = 1.0 / math.sqrt(96.0)
            kbs = [(0, NMETA, 0, t == 0)]
            if t > 0:
                kbs += [(1 + j, 128, 0, False) for j in range(4 * (t - 1))]
                kbs += [(1 + 4 * (t - 1) + r_, 128, 128 * r_, True) for r_ in range(4)]
            sring = Ring([0, 1, 2])
            pring = Ring([0, 1, 2, 3])
            for hp in range(2):
                pv_pend = []
                for bi, (kb, nk, c0, diag) in enumerate(kbs):
                    s = wstate.setdefault("kv", 0) % NKV
                    wstate["kv"] += 1
                    tk = 0 if kb == 0 else 1 + (kb - 1) // 4
                    k.dma(sp, kslot[s][:, :, :], kT_d[kb][:, 4 * hp:4 * hp + 4, :], reads=[b_kvd[tk]], writes=[b_kslot[s]], sem=ksem[s])
                    k.dma(sp, vslot[s][0:nk], v_d[kb][0:nk, 4 * hp:4 * hp + 4, :], reads=[b_kvd[tk]], writes=[b_vslot[s]], sem=vsem[s])
                    for hh in range(4):
                        h = 4 * hp + hh
                        si = sring.next()
                        sbk, bsb = banks[si], b_banks[si]
                        k.op(pe, lambda s=s, hh=hh, h=h, sbk=sbk: nc.tensor.matmul(sbk[0:nk, c0:N], lhsT=kslot[s][:, hh, 0:nk], rhs=qT[:, h, c0:N], start=True, stop=True),
                             reads=[b_kslot[s], b_qT[h]], writes=[bsb])
                        pi = pring.next()
                        k.op(act, lambda pi=pi, sbk=sbk: nc.scalar.activation(out=pbuf[pi][0:nk, c0:N], in_=sbk[0:nk, c0:N], func=AF.Exp, scale=scale),
                             reads=[bsb], writes=[b_pbuf[pi]])
                        if diag:
                            wd = min(128, N - c0)
                            k.op(dve, lambda pi=pi, wd=wd: nc.vector.tensor_tensor(out=pbuf[pi][0:nk, c0:c0 + wd], in0=pbuf[pi][0:nk, c0:c0 + wd],
                                                                                    in1=tri_bf[0:nk, 0:wd], op=ALU.mult),
                                 reads=[b_pbuf[pi], b_tri], writes=[b_pbuf[pi]])
                        ob, bob = banks[3 + hh], b_banks[3 + hh]

                        def pv(s=s, hh=hh, pi=pi, ob=ob, bob=bob, nk=nk, c0=c0, bi=bi):
                            k.op(pe, lambda: nc.tensor.matmul(ob[:, c0:N], lhsT=vslot[s][0:nk, hh, :], rhs=pbuf[pi][0:nk, c0:N],
                                                              start=(bi == 0), stop=(bi == len(kbs) - 1)),
                                 reads=[b_vslot[s], b_pbuf[pi]], writes=[bob])
                        pv_pend.append(pv)
                        if len(pv_pend) > 3:
                            pv_pend.pop(0)()
                while pv_pend:
                    pv_pend.pop(0)()
                for hh in range(4):
                    h = 4 * hp + hh
                    ob, bob = banks[3 + hh], b_banks[3 + hh]
                    pr, par = h // 2, h % 2
                    orow = slice(0, 64) if par == 0 else slice(64, 128)
                    drow = slice(64, 128) if par == 0 else slice(0, 64)
                    ri = hh % 2
                    k.op(act, lambda ob=ob, ri=ri: nc.scalar.activation(out=rec[ri][drow, 0:N], in_=ob[drow, 0:N], func=AF.Ln), reads=[bob], writes=[b_rec[ri]])
                    k.op(act, lambda ri=ri: nc.scalar.activation(out=rec[ri][orow, 0:N], in_=rec[ri][drow, 0:N], func=AF.Exp, scale=-1.0),
                         reads=[b_rec[ri]], writes=[b_rec[ri]])
                    k.op(dve, lambda ob=ob, ri=ri: nc.vector.tensor_tensor(out=otmp[ri][orow, 0:N], in0=ob[orow, 0:N], in1=rec[ri][orow, 0:N], op=ALU.mult),
                         reads=[bob, b_rec[ri]], writes=[b_otmp[ri]])
                    k.op(pool, lambda ri=ri, pr=pr: nc.gpsimd.tensor_tensor(out=om[orow, pr, 0:N], in0=otmp[ri][orow, 0:N], in1=zm[orow, pr, 0:N], op=ALU.mult),
                         reads=[b_otmp[ri], b_zm[pr]], writes=[b_om[pr]])

            if stage < 3.45:
                return None
            for g in range(4):
                bank, bb, _, _ = proj(l, "u%d" % g, 128, N, h_rhs, b_hT)
                k.op(dve, lambda g=g, bank=bank: nc.vector.tensor_copy(out=u_t[:, g, 16:16 + N], in_=bank[:, 0:N]), reads=[bb], writes=[b_u[g]])
            for g in range(4):
                bank, bb, _, _ = proj(l, "zp%d" % g, 128, N, h_rhs, b_hT)
                k.op(act, lambda g=g, bank=bank: nc.scalar.activation(out=zp[:, g, 0:N], in_=bank[:, 0:N], func=AF.Silu), reads=[bb], writes=[b_zp[g]])
            wpgt, bwpg = wload(l, "pg")
            wpgv = wpgt[:, 0:512].rearrange("p (g m) -> p g m", m=128)
            for g in range(4):
                W_ = 16 + N
                src, bsrc = u_t[:, g, 0:W_], [b_u[g], b_uh[g]]
                sh = 1
                pi_ = 0
                while sh < POOL_W[g]:
                    dst, bdst = pt[pi_], b_pt[pi_]
                    k.op(pool, lambda src=src, dst=dst, sh=sh: nc.gpsimd.tensor_tensor(out=dst[:, sh:W_], in0=src[:, sh:W_], in1=src[:, 0:W_ - sh], op=ALU.add),
                         reads=bsrc, writes=[bdst])
                    src, bsrc = dst[:, 0:W_], [bdst]
                    sh *= 2
                    pi_ ^= 1
                if t == 0:
                    k.op(dve, lambda src=src, g=g, pi_=pi_: nc.vector.tensor_tensor(out=pt[pi_][:, 16:16 + N], in0=src[:, 16:16 + N], in1=cst2_t[:, g, :], op=ALU.mult),
                         reads=bsrc + [b_cst], writes=[b_pt[pi_]])
                    k.op(dve, lambda g=g, pi_=pi_: nc.vector.tensor_tensor(out=mixed[g][:, 0:N], in0=pt[pi_][:, 16:16 + N], in1=u_t[:, g, 16:16 + N], op=ALU.subtract),
                         reads=[b_pt[pi_], b_u[g]], writes=[b_mixed[g]])
                else:
                    k.op(dve, lambda src=src, g=g: nc.vector.scalar_tensor_tensor(out=mixed[g][:, 0:N], in0=src[:, 16:16 + N], scalar=1.0 / POOL_W[g],
                                                                                 in1=u_t[:, g, 16:16 + N], op0=ALU.mult, op1=ALU.subtract),
                         reads=bsrc + [b_u[g]], writes=[b_mixed[g]])
                k.op(pool, lambda g=g: nc.gpsimd.tensor_copy(out=u_t[:, g, 0:16], in_=u_t[:, g, N:N + 16]), reads=[b_u[g]], writes=[b_uh[g]])
                bank, bb = gbank()
                k.op(pe, lambda g=g, bank=bank: nc.tensor.matmul(bank[:, 0:N], lhsT=wpgv[:, g, :], rhs=mixed[g][:, 0:N], start=True, stop=True),
                     reads=[bwpg, b_mixed[g]], writes=[bb])
                k.op(dve, lambda g=g, bank=bank: nc.vector.scalar_tensor_tensor(out=pg[:, g, 0:N], in0=bank[:, 0:N], scalar=psc_t[:, l, g:g + 1],
                                                                               in1=zp[:, g, 0:N], op0=ALU.mult, op1=ALU.mult),
                     reads=[bb, b_zp[g], b_gains], writes=[b_pg[g]])

            if stage < 3.55:
                return None
            if last and t == 0 and not meta_out:
                return

            pg_rhs = [pg[:, c, 0:N] for c in range(4)]
            om_rhs = [om[:, c, 0:N] for c in range(4)]
            sgr = Ring([0, 1, 2, 3])
            mtr = Ring([0, 1])
            for m in range(8):
                b1, bb1, _, _ = proj(l, "gp%d" % m, 128, N, h_rhs, b_hT)
                s1 = sgr.next()
                k.op(act, lambda b1=b1, s1=s1: nc.scalar.activation(out=sg[s1][:, 0:N], in_=b1[:, 0:N], func=AF.Sigmoid), reads=[bb1], writes=[b_sg[s1]])
                b2, bb2, _, _ = proj(l, "gm%d" % m, 128, N, h_rhs, b_hT)
                s2 = sgr.next()
                k.op(act, lambda b2=b2, s2=s2: nc.scalar.activation(out=sg[s2][:, 0:N], in_=b2[:, 0:N], func=AF.Sigmoid), reads=[bb2], writes=[b_sg[s2]])
                w, bw = wload(l, "up%d" % m)
                wu = w[:, 0:1024].rearrange("p (t c m) -> p t c m", t=2, m=128)
                b3, bb3 = gbank()
                for c in range(4):
                    k.op(pe, lambda c=c, b3=b3: nc.tensor.matmul(b3[:, 0:N], lhsT=wu[:, 0, c, :], rhs=pg_rhs[c], start=(c == 0), stop=(c == 3)),
                         reads=[bw, b_pg[c]], writes=[bb3])
                m1 = mtr.next()
                k.op(dve, lambda b3=b3, s1=s1, m1=m1: nc.vector.tensor_tensor(out=mt[m1][:, 0:N], in0=b3[:, 0:N], in1=sg[s1][:, 0:N], op=ALU.mult),
                     reads=[bb3, b_sg[s1]], writes=[b_mt[m1]])
                b4, bb4 = gbank()
                for c in range(4):
                    k.op(pe, lambda c=c, b4=b4: nc.tensor.matmul(b4[:, 0:N], lhsT=wu[:, 1, c, :], rhs=om_rhs[c], start=(c == 0), stop=(c == 3)),
                         reads=[bw, b_om[c]], writes=[bb4])
                m2 = mtr.next()
                k.op(dve, lambda b4=b4, s2=s2, m2=m2: nc.vector.tensor_tensor(out=mt[m2][:, 0:N], in0=b4[:, 0:N], in1=sg[s2][:, 0:N], op=ALU.mult),
                     reads=[bb4, b_sg[s2]], writes=[b_mt[m2]])
                k.op(pool, lambda m=m, m1=m1, m2=m2: nc.gpsimd.tensor_tensor(out=merged[:, m, 0:N], in0=mt[m1][:, 0:N], in1=mt[m2][:, 0:N], op=ALU.add),
                     reads=[b_mt[m1], b_mt[m2]], writes=[b_merged[m]])

            if stage < 3.65:
                return None
            mg_rhs = [merged[:, c, 0:N] for c in range(8)]
            for m in range(8):
                bank, bb, _, _ = proj(l, "wo%d" % m, 128, N, mg_rhs, b_merged)
                k.op(dve, lambda m=m, bank=bank: nc.vector.tensor_tensor(out=x_t[:, m, 0:N], in0=x_t[:, m, 0:N], in1=bank[:, 0:N], op=ALU.add),
                     reads=[bb, bx[m]], writes=[bx[m]])
            if last and t == 0:
                return k.dma(pool, metaT_out.rearrange("(c p) n -> p c n", p=128), x_t[:, :, 0:N], reads=bx, writes=[b_xs_meta])
            if last:
                return k.dma(pool, yT.rearrange("(c p) n -> p c n", p=128)[:, :, (t - 1) * TW:t * TW], x_t[:, :, 0:N], reads=bx, writes=[b_xs[t - 1]])
            if t == 0:
                k.dma(pool, xs_meta[:, :, :], x_t[:, :, 0:N], reads=bx, writes=[b_xs_meta])
            else:
                k.dma(pool, xs[t - 1], x_t[:, :, 0:N], reads=bx, writes=[b_xs[t - 1]])
            return None

        cst2 = din("cst2", [128, 4, NMETA])
        cst2_t = sb("cst2_t", [128, 4, NMETA], F32)
        ld(sp, cst2_t[:], cst2[:, :, :], writes=[b_cst])

        if stage >= 1:
            emit_casts(0)
        if stage >= 2:
            rope_tables()
        out_evs = []
        pending_loads = [None]
        for l in range(n_layers if stage >= 3 else 0):
            if l > 0:
                k.epoch()
            if l + 1 < n_layers:
                emit_casts(l + 1)
            for t in range(n_tiles + 1):
                gi = l * (n_tiles + 1) + t
                if gi == 0:
                    emit_loads(0, 0, 0)
                nxt = (l, t + 1) if t < n_tiles else ((l + 1, 0) if l + 1 < n_layers else None)
                pending_loads[0] = (lambda nxt=nxt, gi=gi: emit_loads(nxt[0], nxt[1], (gi + 1) % 2)) if nxt else None
                ev = tile_layer(l, t, gi % 2)
                if pending_loads[0] is not None:
                    pending_loads[0]()
                    pending_loads[0] = None
                if ev is not None:
                    out_evs.append(ev)
        for ev in out_evs:
            pool.wait(ev)
        for sm in k.all_sems:
            if sm.cnt:
                pool.wait((sm, sm.cnt))
    return nc


def make_consts():
    cst = np.zeros((128, 160), np.float32)
    p = np.arange(128)[:, None]
    c = np.arange(128)[None, :]
    cst[:, 0:128] = (p <= c).astype(np.float32)
    i = np.arange(32) % 16
    cst[0:32, 128] = (10000.0 ** (-(i.astype(np.float32)) / 16.0)).astype(np.float32)
    cst[0:32, 129] = np.where(np.arange(32) < 16, -1.0, 1.0)
    cst[0:32, 130] = np.where(np.arange(32) < 16, math.pi, -math.pi)
    cst[0:32, 131] = -math.pi
    cst[:, 132:148] = np.arange(16, dtype=np.float32)[None, :]
    cst[:, 148] = EPS
    cst2 = np.zeros((128, 4, NMETA), np.float32)
    for g, w in enumerate(POOL_W):
        cst2[:, g, :] = 1.0 / np.minimum(np.arange(1, NMETA + 1), w).astype(np.float32)[None, :]
    return cst, cst2


_CACHE = {}


def run(inputs, n_layers=DEPTH, n_tiles=SEQ // TW, stage=3.7):
    key = (n_layers, n_tiles, stage)
    if key not in _CACHE:
        _CACHE[key] = build(n_layers, n_tiles, stage)
    nc = _CACHE[key]
    S = n_tiles * TW
    cst, cst2 = make_consts()
    x = np.asarray(inputs["x"], np.float32)
    B = x.shape[0]
    shared = {
        "metaT": np.ascontiguousarray(np.asarray(inputs["meta_tokens"], np.float32).T),
        "cst": cst, "cst2": cst2,
    }
    for n in ("norm_gain", "w_in", "pool_w_group", "pool_scale", "pool_w_up", "q_a_norm_gain", "kv_a_norm_gain",
              "w_q_b", "w_kv_b", "q_norm_gain", "k_norm_gain", "mla_w_up", "w_out"):
        shared[n] = np.ascontiguousarray(np.asarray(inputs[n], np.float32)[:n_layers])
    in_maps = []
    for c in range(8):
        b = c % B
        m = dict(shared)
        m["xT"] = np.ascontiguousarray(x[b, :S].T)
        m["pos"] = np.ascontiguousarray(np.asarray(inputs["positions"], np.int32)[b:b + 1, :S])
        in_maps.append(m)
    res = run_bass_kernel_spmd(nc, in_maps, core_ids=list(range(8)))
    out = np.stack([np.ascontiguousarray(res.results[b]["yT"].T) for b in range(B)], axis=0)
    return out.astype(np.float32)


def run_unfused(inputs):
    lpl = LAYERS_PER_LAUNCH
    key = ("unfused", lpl)
    if key not in _CACHE:
        _CACHE[key] = build(lpl, SEQ // TW, meta_out=True)
    nc = _CACHE[key]
    cst, cst2 = make_consts()
    x = np.asarray(inputs["x"], np.float32)
    B = x.shape[0]
    xT = [np.ascontiguousarray(x[b].T) for b in range(B)]
    metaT = [np.ascontiguousarray(np.asarray(inputs["meta_tokens"], np.float32).T) for _ in range(B)]
    pos = np.asarray(inputs["positions"], np.int32)
    names = ("norm_gain", "w_in", "pool_w_group", "pool_scale", "pool_w_up", "q_a_norm_gain", "kv_a_norm_gain",
             "w_q_b", "w_kv_b", "q_norm_gain", "k_norm_gain", "mla_w_up", "w_out")
    for l in range(0, DEPTH, lpl):
        shared = {"cst": cst, "cst2": cst2}
        for n in names:
            shared[n] = np.ascontiguousarray(np.asarray(inputs[n], np.float32)[l:l + lpl])
        in_maps = []
        for c in range(8):
            b = c % B
            m = dict(shared)
            m["xT"] = xT[b]
            m["metaT"] = metaT[b]
            m["pos"] = np.ascontiguousarray(pos[b:b + 1])
            in_maps.append(m)
        res = run_bass_kernel_spmd(nc, in_maps, core_ids=list(range(8)))
        xT = [np.ascontiguousarray(res.results[b]["yT"]) for b in range(B)]
        metaT = [np.ascontiguousarray(res.results[b]["metaT_out"]) for b in range(B)]
    return np.stack([np.ascontiguousarray(xT[b].T) for b in range(B)], axis=0).astype(np.float32)


FUSED = True
LAYERS_PER_LAUNCH = 2


def kernel(**inputs):
    return run(inputs) if FUSED else run_unfused(inputs)
```
